# Optimizing a Trainium2 kernel written in Bass

```python
import jax, jax.numpy as jnp
from jax import lax
import numpy as np

D_MODEL = 1024
BATCH = 32
SEQ = 2048
DEPTH = 2
DEC_BATCH = 2
DEC_SEQ = 8192
PAST_LEN = 128

GRID_W = 64
ROPE_THETA = 10000.0
Q_BLOCK = 128
EPS = 1e-6

H_A = 8
KV_A = 2
HD_A = 64
H_B = 8
Q_LORA = 384
KV_LORA = 256
NOPE_B = 64
ROPE_B = 32
V_B = 64
QK_B = NOPE_B + ROPE_B

D_FF = 2816
CONV_W = 3

W_QA = H_A * HD_A
W_KA = KV_A * HD_A
W_OB = H_B * V_B
N_IN = W_QA + 2 * W_KA + Q_LORA + KV_LORA + ROPE_B + 2 * D_MODEL
SPLIT_POINTS = tuple(int(v) for v in np.cumsum([W_QA, W_KA, W_KA, Q_LORA, KV_LORA, ROPE_B]))

kernel_name = "hybrid_gqa_mla_gated_encoder"


def _rmsnorm(x, g):
    xf = x.astype(jnp.float32)
    y = xf * lax.rsqrt(jnp.mean(xf * xf, axis=-1, keepdims=True) + EPS)
    return (y * g.astype(jnp.float32)).astype(x.dtype)


def _axial_angles(seq_len, rot_dim):
    rows = seq_len // GRID_W
    row_ids = jnp.repeat(jnp.arange(rows, dtype=jnp.float32), GRID_W)
    col_ids = jnp.tile(jnp.arange(GRID_W, dtype=jnp.float32), rows)
    half = rot_dim // 2
    inv = ROPE_THETA ** (-jnp.arange(0, half, 2, dtype=jnp.float32) / half)
    ang = jnp.concatenate([row_ids[:, None] * inv, col_ids[:, None] * inv], axis=-1)
    return jnp.cos(ang), jnp.sin(ang)


def _apply_rope(x, cos, sin):
    xf = x.astype(jnp.float32).reshape(x.shape[:-1] + (x.shape[-1] // 2, 2))
    x0, x1 = xf[..., 0], xf[..., 1]
    c = cos[None, :, None, :]
    s = sin[None, :, None, :]
    out = jnp.stack([x0 * c - x1 * s, x0 * s + x1 * c], axis=-1).reshape(x.shape)
    return out.astype(x.dtype)


def _block_attention(q, k, v, scale):
    B, S, H, dk = q.shape
    Hkv = k.shape[2]
    dv = v.shape[-1]
    G = H // Hkv
    nb = S // Q_BLOCK
    qb = q.reshape(B, nb, Q_BLOCK, Hkv, G, dk).transpose(1, 0, 2, 3, 4, 5)
    kf = k.astype(jnp.float32)
    vf = v.astype(jnp.float32)

    def one_block(qi):
        s = jnp.einsum('bqkgd,bskd->bkgqs', qi.astype(jnp.float32), kf) * scale
        p = jax.nn.softmax(s, axis=-1)
        return jnp.einsum('bkgqs,bskd->bqkgd', p, vf).astype(v.dtype)

    out = lax.map(one_block, qb)
    return out.transpose(1, 0, 2, 3, 4, 5).reshape(B, S, H, dv)


def _dwconv(x, w, b):
    C = x.shape[-1]
    y = lax.conv_general_dilated(
        x, w[:, None, :].astype(x.dtype), window_strides=(1,),
        padding=[((CONV_W - 1) // 2, (CONV_W - 1) // 2)],
        dimension_numbers=('NWC', 'WIO', 'NWC'), feature_group_count=C)
    return y + b.astype(x.dtype)


def _layer(x, cos_a, sin_a, cos_b, sin_b,
           g_mix_pre, w_in, g_qa, g_ka, g_cq, w_uq, g_ckv, w_ukv,
           w_oa, w_ob, b_gates, w_out, g_mix_post,
           g_ffn_pre, w_up, conv_w, conv_b, w_down, g_ffn_post):
    B, S, _ = x.shape
    h = _rmsnorm(x, g_mix_pre)
    proj = h @ w_in
    qa, ka, va, cq, ckv, kr, gates = jnp.split(proj, SPLIT_POINTS, axis=-1)

    qa = _apply_rope(_rmsnorm(qa.reshape(B, S, H_A, HD_A), g_qa), cos_a, sin_a)
    ka = _apply_rope(_rmsnorm(ka.reshape(B, S, KV_A, HD_A), g_ka), cos_a, sin_a)
    va = va.reshape(B, S, KV_A, HD_A)
    ya = _block_attention(qa, ka, va, HD_A ** -0.5).reshape(B, S, W_QA) @ w_oa

    qb = (_rmsnorm(cq, g_cq) @ w_uq).reshape(B, S, H_B, QK_B)
    qb = jnp.concatenate([qb[..., :NOPE_B], _apply_rope(qb[..., NOPE_B:], cos_b, sin_b)], axis=-1)
    kvb = (_rmsnorm(ckv, g_ckv) @ w_ukv).reshape(B, S, H_B, NOPE_B + V_B)
    kr = _apply_rope(kr.reshape(B, S, 1, ROPE_B), cos_b, sin_b)
    kb = jnp.concatenate([kvb[..., :NOPE_B], jnp.broadcast_to(kr, (B, S, H_B, ROPE_B))], axis=-1)
    vb = kvb[..., NOPE_B:]
    yb = _block_attention(qb, kb, vb, QK_B ** -0.5).reshape(B, S, W_OB) @ w_ob

    ga, gb = jnp.split(jax.nn.sigmoid(gates + b_gates), 2, axis=-1)
    mixed = (ga * ya + gb * yb) @ w_out
    x = x + _rmsnorm(mixed, g_mix_post)

    h = _rmsnorm(x, g_ffn_pre)
    u = _dwconv(h @ w_up, conv_w, conv_b)
    gate, val = jnp.split(u, 2, axis=-1)
    f = (jax.nn.gelu(gate, approximate=True) * val) @ w_down
    return x + _rmsnorm(f, g_ffn_post)


def _trunk(x, g_mix_pre, w_in, g_qa, g_ka, g_cq, w_uq, g_ckv, w_ukv,
           w_oa, w_ob, b_gates, w_out, g_mix_post,
           g_ffn_pre, w_up, conv_w, conv_b, w_down, g_ffn_post):
    S = x.shape[1]
    cos_a, sin_a = _axial_angles(S, HD_A)
    cos_b, sin_b = _axial_angles(S, ROPE_B)
    for l in range(DEPTH):
        x = _layer(x, cos_a, sin_a, cos_b, sin_b,
                   g_mix_pre[l], w_in[l], g_qa[l], g_ka[l], g_cq[l], w_uq[l], g_ckv[l], w_ukv[l],
                   w_oa[l], w_ob[l], b_gates[l], w_out[l], g_mix_post[l],
                   g_ffn_pre[l], w_up[l], conv_w[l], conv_b[l], w_down[l], g_ffn_post[l])
    return x


def setup_inputs(seed: int = 0) -> dict:
    key = jax.random.key(seed)
    ks = jax.random.split(key, 24)
    f32 = jnp.float32

    def w(k, shape, fan_in):
        return jax.random.normal(k, shape, f32) * (fan_in ** -0.5)

    def gain(k, shape):
        return 1.0 + 0.05 * jax.random.normal(k, shape, f32)

    L = DEPTH
    return {
        "x_prompt": jax.random.normal(ks[0], (BATCH, SEQ, D_MODEL), f32),
        "x_sample": jax.random.normal(ks[1], (DEC_BATCH, DEC_SEQ, D_MODEL), f32),
        "g_mix_pre": gain(ks[2], (L, D_MODEL)),
        "w_in": w(ks[3], (L, D_MODEL, N_IN), D_MODEL),
        "g_qa": gain(ks[4], (L, HD_A)),
        "g_ka": gain(ks[5], (L, HD_A)),
        "g_cq": gain(ks[6], (L, Q_LORA)),
        "w_uq": w(ks[7], (L, Q_LORA, H_B * QK_B), Q_LORA),
        "g_ckv": gain(ks[8], (L, KV_LORA)),
        "w_ukv": w(ks[9], (L, KV_LORA, H_B * (NOPE_B + V_B)), KV_LORA),
        "w_oa": w(ks[10], (L, W_QA, D_MODEL), W_QA),
        "w_ob": w(ks[11], (L, W_OB, D_MODEL), W_OB),
        "b_gates": 0.1 * jax.random.normal(ks[12], (L, 2 * D_MODEL), f32),
        "w_out": w(ks[13], (L, D_MODEL, D_MODEL), D_MODEL),
        "g_mix_post": gain(ks[14], (L, D_MODEL)),
        "g_ffn_pre": gain(ks[15], (L, D_MODEL)),
        "w_up": w(ks[16], (L, D_MODEL, 2 * D_FF), D_MODEL),
        "conv_w": w(ks[17], (L, CONV_W, 2 * D_FF), CONV_W),
        "conv_b": 0.01 * jax.random.normal(ks[18], (L, 2 * D_FF), f32),
        "w_down": w(ks[19], (L, D_FF, D_MODEL), D_FF),
        "g_ffn_post": gain(ks[20], (L, D_MODEL)),
    }


def reference(x_prompt, x_sample, g_mix_pre, w_in, g_qa, g_ka, g_cq, w_uq, g_ckv, w_ukv,
              w_oa, w_ob, b_gates, w_out, g_mix_post,
              g_ffn_pre, w_up, conv_w, conv_b, w_down, g_ffn_post):
    y_prompt = _trunk(x_prompt, g_mix_pre, w_in, g_qa, g_ka, g_cq, w_uq, g_ckv, w_ukv,
                      w_oa, w_ob, b_gates, w_out, g_mix_post,
                      g_ffn_pre, w_up, conv_w, conv_b, w_down, g_ffn_post)
    y_sample = _trunk(x_sample, g_mix_pre, w_in, g_qa, g_ka, g_cq, w_uq, g_ckv, w_ukv,
                      w_oa, w_ob, b_gates, w_out, g_mix_post,
                      g_ffn_pre, w_up, conv_w, conv_b, w_down, g_ffn_post)
    return (y_prompt, y_sample)
```

```python
import os
import numpy as np
import concourse.bass as bass
import concourse.mybir as mybir
from concourse.bass_utils import run_bass_kernel_spmd

F32 = mybir.dt.float32
BF16 = mybir.dt.bfloat16
U8 = mybir.dt.uint8
AF = mybir.ActivationFunctionType
ALU = mybir.AluOpType

NCORES = 8
D = 1024
T = 2048
NTB = 4
TB = 512
NUNITS = 5
L = 2
EPS = 1e-6
DFF = 2816
NCH = 22
NG_IN = 34
NV = 233
GROWS = 544
SCALE_A = 64 ** -0.5
SCALE_B = 96 ** -0.5

V_GPRE, V_GQ, V_GQS, V_GK, V_GKS, V_GCQ, V_GCKV, V_BG, V_GPOST, V_GFPRE, V_CW, V_CB, V_GFPOST = (
    0, 8, 9, 10, 11, 12, 15, 17, 33, 41, 49, 181, 225)


_REGS = {}


def Region(name):
    if name not in _REGS:
        _REGS[name] = _Region(name)
    return _REGS[name]


class _Region:
    __slots__ = ("name", "w", "r", "sem", "persist")

    def __init__(self, name):
        self.name = name
        self.w = None
        self.r = {}
        self.sem = None
        self.persist = False


class Eng:
    def __init__(self, name, h, sem, is_pe=False):
        self.name = name
        self.h = h
        self.sem = sem
        self.cnt = 0
        self.seen = {}
        self.is_pe = is_pe


class Tracker:
    def __init__(self, nc):
        self.nc = nc
        self._sems = []
        self.E = {}
        for name, h in (("pe", nc.tensor), ("act", nc.scalar), ("dve", nc.vector),
                        ("pool", nc.gpsimd), ("sp", nc.sync)):
            self.E[name] = Eng(name, h, self.new_sem("e_" + name), is_pe=(name == "pe"))
        self.touched = set()
        self.n_inst = 0
        self.n_wait = 0
        self.pool = []
        self.pool_used = 0
        self.semcnt = {}
        self.pooled_regs = []
        self.outq = {}
        self.max_out = int(os.environ.get("KDBG_MAXOUT", "2"))

    def new_sem(self, name):
        g = self.nc.semaphore(name)
        s = g.__enter__()
        self._sems.append(g)
        return s

    def _wait(self, e, deps):
        for sem, val in deps.items():
            if e.seen.get(sem, 0) < val:
                e.h.wait_ge(sem, val)
                e.seen[sem] = val
                self.n_wait += 1

    def _deps(self, e, r, w):
        deps = {}

        def add(ev):
            if ev is None:
                return
            s, v = ev
            if e.is_pe and s is e.sem:
                return
            if deps.get(s, 0) < v:
                deps[s] = v
        for reg in r:
            add(reg.w)
        for reg in w:
            add(reg.w)
            for s, v in reg.r.items():
                add((s, v))
        return deps

    def _commit(self, ev, r, w):
        s, v = ev
        for reg in w:
            reg.w = ev
            reg.r = {}
            self.touched.add(reg)
        for reg in r:
            if reg.r.get(s, 0) < v:
                reg.r[s] = v
            self.touched.add(reg)

    def op(self, eng, fn, r=(), w=()):
        e = self.E[eng]
        self._wait(e, self._deps(e, r, w))
        ins = fn(e.h)
        e.cnt += 1
        ins.then_inc(e.sem, 1)
        self.n_inst += 1
        self._commit((e.sem, e.cnt), r, w)

    def dma(self, eng, out, in_, r=(), w=(), semreg=None):
        e = self.E[eng]
        self._wait(e, self._deps(e, r, w))
        reg = semreg if semreg is not None else w[0]
        if reg.sem is None:
            if reg.persist:
                reg.sem = self.new_sem("d_" + reg.name)
            else:
                if self.pool_used == len(self.pool):
                    self.pool.append(self.new_sem(f"dp{len(self.pool)}"))
                reg.sem = self.pool[self.pool_used]
                self.pool_used += 1
                self.pooled_regs.append(reg)
        fifo = self.outq.setdefault(eng, [])
        while len(fifo) >= self.max_out:
            s_, v_ = fifo.pop(0)
            self._wait(e, {s_: v_})
        ins = e.h.dma_start(out=out, in_=in_)
        v = self.semcnt.get(reg.sem, 0) + 16
        self.semcnt[reg.sem] = v
        ins.then_inc(reg.sem, 16)
        self.n_inst += 1
        fifo.append((reg.sem, v))
        self._commit((reg.sem, v), r, w)

    def custom(self, eng, fn, sem, r=(), w=()):
        e = self.E[eng]
        self._wait(e, self._deps(e, r, w))
        ins = fn(e.h)
        ins.then_inc(sem)
        self._commit((sem, 1), r, w)

    def barrier(self):
        deps = {}
        keep = set()
        for reg in self.touched:
            if reg.persist:
                keep.add(reg)
                continue
            evs = list(reg.r.items())
            if reg.w is not None:
                evs.append(reg.w)
            for s, v in evs:
                if deps.get(s, 0) < v:
                    deps[s] = v
            reg.w = None
            reg.r = {}
        self.touched = keep
        for e in self.E.values():
            d = {s: v for s, v in deps.items() if s is not e.sem}
            self._wait(e, d)
        for reg in self.pooled_regs:
            reg.sem = None
        self.pooled_regs = []
        self.pool_used = 0


class Arena:
    def __init__(self, nc, nbytes):
        self.t = nc.alloc_sbuf_tensor("arena", [128, nbytes], U8).ap()
        self.nbytes = nbytes
        self.off = 0

    def mark(self):
        return self.off

    def reset(self, m):
        self.off = m

    def alloc(self, shape, dt, at=None):
        esz = 4 if dt == F32 else 2
        n = int(np.prod(shape[1:])) * esz
        if at is None:
            self.off = (self.off + 31) // 32 * 32
            assert self.off + n <= self.nbytes, ("arena overflow", self.off, n, self.nbytes)
            o = self.off
            self.off += n
        else:
            o = at
        ap = self.t[:, o:o + n].bitcast(dt)
        if len(shape) == 3:
            ap = ap.rearrange("p (a b) -> p a b", a=shape[1])
        elif len(shape) == 4:
            ap = ap.rearrange("p (a b c) -> p a b c", a=shape[1], b=shape[2])
        return ap


def build_program(n_units=NUNITS, n_layers=L, units=None, unit_layers=None):
    nc = bass.Bass("TRN2", target_bir_lowering=False)
    _REGS.clear()
    tk = Tracker(nc)
    op, dma = tk.op, tk.dma
    import os
    WQ = os.environ.get("KDBG_WQ", "act")
    unit_list = list(range(n_units)) if units is None else units

    def din(name, shape, dt=F32):
        return nc.dram_tensor(name, list(shape), dt, kind="ExternalInput").ap()

    xin = din("xin", [NUNITS, 128, 8 * T])
    pidx = din("pidx", [128, 8])
    tabs = nc.dram_tensor("tabs_d", [2, 2, NTB, 128, 2 * TB], F32).ap()
    R_tabs = Region("tabs_d")
    R_tabs.persist = True
    vecs = din("vecs", [L, 128, NV])
    nbr = din("nbr", [128, 8])
    WSH = {"in": (NG_IN * 128, 1024), "uq": (8 * 128, 576), "ukv": (8 * 128, 256), "oa": (8 * 128, 512),
           "ob": (8 * 128, 512), "out": (8 * 128, 1024), "up": (2 * NCH * 128, 1024), "down": (8 * 128, NCH * 128)}
    WF = {k: din("w_" + k, [L, r, c]) for k, (r, c) in WSH.items()}
    yout = nc.dram_tensor("yout", [NUNITS, 128, 8 * T], F32, kind="ExternalOutput").ap()

    def dscr(name, shape, dt=BF16):
        return nc.dram_tensor(name, list(shape), dt).ap()

    WB = {k: dscr("b_" + k, [L, r, c]) for k, (r, c) in WSH.items()}

    def wsrc(name, l, g, k):
        return WB[name][l, g * 128:(g + 1) * 128, :].rearrange("p (k m) -> p k m", k=k)
    GP = [(128, T), (256, T), (32, T), (128, 16 * 193)]
    NGP = len(GP)
    gin = [[dscr(f"gin{l}_{p}", [GP[p][0], GP[p][1]]) for p in range(NGP)] for l in range(L)]
    gout = [[dscr(f"gout{l}_{p}", [4 * GP[p][0], GP[p][1]]) for p in range(NGP)] for l in range(L)]
    R_gin = [[Region(f"gin{l}_{p}") for p in range(NGP)] for l in range(L)]
    R_gout = [[Region(f"gout{l}_{p}") for p in range(NGP)] for l in range(L)]
    cc_sems = [[tk.new_sem(f"cc{l}_{p}") for p in range(NGP)] for l in range(L)]
    R_yout = Region("yout")
    hgin = [dscr(f"hgin{l}", [128, 16]) for l in range(L)]
    hgout = [dscr(f"hgout{l}", [512, 16]) for l in range(L)]
    R_hgin = [Region(f"hgin{l}") for l in range(L)]
    R_hgout = [Region(f"hgout{l}") for l in range(L)]
    cc_h = [tk.new_sem(f"cch{l}") for l in range(L)]

    R_w = {}

    def convert(name, l, rows_per):
        reg = Region(f"cv_{name}{l}")
        reg.persist = True
        R_w[(name, l)] = reg
        n0 = WSH[name][0]
        for i in range(0, n0, rows_per):
            j = min(n0, i + rows_per)
            dma("pool", WB[name][l, i:j, :], WF[name][l, i:j, :], w=[reg])

    for l in range(n_layers):
        convert("in", l, 512)
        convert("uq", l, 512)
        convert("ukv", l, 1024)
        convert("oa", l, 1024)
        convert("ob", l, 1024)
        convert("out", l, 512)
        convert("up", l, 512)
        convert("down", l, 256)

    PB = [nc.alloc_psum_tensor(f"pb{i}", [128, 512], F32).ap() for i in range(8)]
    R_PB = [Region(f"pb{i}") for i in range(8)]

    ar = Arena(nc, 204 * 1024)
    ones_bf = ar.alloc([128, 128], BF16)
    bd_bf = ar.alloc([128, 128], BF16)
    ones_lo = ar.alloc([128, 64], F32)
    ones_hi = ar.alloc([128, 128], F32)
    epsc = ar.alloc([128, 1], F32)
    vec_sb = ar.alloc([128, L, NV], F32)
    nbr_sb = ar.alloc([128, 8], F32)
    hx = ar.alloc([128, 16], BF16)
    hg = ar.alloc([128, 4, 16], BF16)
    hacc = ar.alloc([128, 2, 8], F32)
    halo_b = ar.alloc([128, 2, 8], BF16)
    R_hx = Region("hx")
    R_hg = Region("hg")
    R_halo = Region("halo")
    R_const = Region("const")
    R_vec = Region("vecs")
    xT = ar.alloc([128, 8, T], F32)
    R_x = [Region(f"x{tb}") for tb in range(NTB)]
    base_mark = ar.mark()

    op("dve", lambda h: h.memset(ones_bf, 1.0), w=[R_const])
    op("dve", lambda h: h.memset(bd_bf, 0.0), w=[R_const])
    op("dve", lambda h: h.memset(bd_bf[0:64, 0:64], 1.0), w=[R_const])
    op("dve", lambda h: h.memset(bd_bf[64:128, 64:128], 1.0), w=[R_const])
    op("dve", lambda h: h.memset(ones_lo, 1.0), w=[R_const])
    op("dve", lambda h: h.memset(ones_hi, 0.0), w=[R_const])
    op("dve", lambda h: h.memset(ones_hi[:, 64:128], 1.0), w=[R_const])
    op("dve", lambda h: h.memset(epsc, EPS), w=[R_const])
    dma("sp", vec_sb, vecs.rearrange("l p v -> p l v"), w=[R_vec])
    dma("sp", nbr_sb, nbr, w=[R_vec])

    for _i in range(int(os.environ.get("KDBG_PEPAD", "0"))):
        op("pe", lambda h: h.matmul(PB[7][:, 0:1], lhsT=ones_bf, rhs=ones_bf[:, 0:1], start=True, stop=True),
           r=[R_const], w=[R_PB[7]])


    def build_tables():
        ar.reset(base_mark)
        R_s = Region("setup")
        pid = ar.alloc([128, 8], F32)
        dma("sp", pid, pidx, w=[R_s])
        C = ar.alloc([128, 8], F32)
        S = ar.alloc([128, 8], F32)
        cl = ar.alloc([128, 32], F32)
        Rc = ar.alloc([128, 128], F32)
        Rs = ar.alloc([128, 128], F32)
        tmp = ar.alloc([128, 128], F32)
        RowC = ar.alloc([128, 32], F32)
        RowS = ar.alloc([128, 32], F32)
        Tc = ar.alloc([128, T], F32)
        Ts = ar.alloc([128, T], F32)
        Tc3 = Tc.rearrange("p (a b) -> p a b", a=32)
        Ts3 = Ts.rearrange("p (a b) -> p a b", a=32)

        def dv(fn):
            op("dve", fn, r=[R_s], w=[R_s])
        HALFPI, ONE = cl[:, 30:31], cl[:, 31:32]
        dv(lambda h: h.memset(HALFPI, float(np.pi / 2)))
        dv(lambda h: h.memset(ONE, 1.0))
        for which in range(2):
            NF = 16 if which == 0 else 8
            kcol, isrow, sign = pid[:, 3 * which:3 * which + 1], pid[:, 3 * which + 1:3 * which + 2], pid[:, 3 * which + 2:3 * which + 3]
            invf = cl[:, 0:1]
            op("act", lambda h: h.activation(out=invf, in_=kcol, func=AF.Exp, scale=float(-np.log(10000.0) / NF)),
               r=[R_s], w=[R_s])
            op("act", lambda h: h.activation(out=S[:, 0:1], in_=invf, func=AF.Sin), r=[R_s], w=[R_s])
            op("act", lambda h: h.activation(out=C[:, 0:1], in_=invf, func=AF.Sin, bias=HALFPI, scale=1.0),
               r=[R_s], w=[R_s])
            for i in range(1, 7):
                a_, b_ = slice(i - 1, i), slice(i, i + 1)
                dv(lambda h: h.tensor_tensor(out=cl[:, 1:2], in0=S[:, a_], in1=S[:, a_], op=ALU.mult))
                dv(lambda h: h.scalar_tensor_tensor(out=C[:, b_], in0=C[:, a_], scalar=C[:, a_], in1=cl[:, 1:2],
                                                    op0=ALU.mult, op1=ALU.subtract))
                dv(lambda h: h.tensor_scalar(out=S[:, b_], in0=S[:, a_], scalar1=C[:, a_], scalar2=2.0,
                                             op0=ALU.mult, op1=ALU.mult))
            dv(lambda h: h.memset(Rc[:, 0:1], 1.0))
            dv(lambda h: h.memset(Rs[:, 0:1], 0.0))
            for i in range(7):
                blk = 1 << i
                lo, hi_ = slice(0, blk), slice(blk, 2 * blk)
                ci, si = C[:, i:i + 1], S[:, i:i + 1]
                dv(lambda h: h.tensor_scalar(out=tmp[:, lo], in0=Rs[:, lo], scalar1=si, scalar2=None, op0=ALU.mult))
                dv(lambda h: h.scalar_tensor_tensor(out=Rc[:, hi_], in0=Rc[:, lo], scalar=ci, in1=tmp[:, lo],
                                                    op0=ALU.mult, op1=ALU.subtract))
                dv(lambda h: h.tensor_scalar(out=tmp[:, lo], in0=Rc[:, lo], scalar1=si, scalar2=None, op0=ALU.mult))
                dv(lambda h: h.scalar_tensor_tensor(out=Rs[:, hi_], in0=Rs[:, lo], scalar=ci, in1=tmp[:, lo],
                                                    op0=ALU.mult, op1=ALU.add))
            nsel, rsg, nsg = cl[:, 2:3], cl[:, 3:4], cl[:, 4:5]
            dv(lambda h: h.tensor_scalar(out=nsel, in0=isrow, scalar1=-1.0, scalar2=1.0, op0=ALU.mult, op1=ALU.add))
            dv(lambda h: h.tensor_tensor(out=rsg, in0=isrow, in1=sign, op=ALU.mult))
            dv(lambda h: h.tensor_tensor(out=nsg, in0=nsel, in1=sign, op=ALU.mult))
            for sel in range(2):
                if sel == 0:
                    rc_, rs_ = Rc[:, 0:32], Rs[:, 0:32]
                else:
                    b5, b6 = pid[:, 6:7], pid[:, 7:8]
                    ca, sa, cb, sb, cq, sq, t_ = (cl[:, 5 + j:6 + j] for j in range(7))
                    for (cx, sx, bx, ii) in ((ca, sa, b5, 5), (cb, sb, b6, 6)):
                        dv(lambda h: h.tensor_scalar(out=cx, in0=C[:, ii:ii + 1], scalar1=-1.0, scalar2=bx,
                                                     op0=ALU.add, op1=ALU.mult))
                        dv(lambda h: h.tensor_scalar(out=cx, in0=cx, scalar1=1.0, scalar2=None, op0=ALU.add))
                        dv(lambda h: h.tensor_tensor(out=sx, in0=S[:, ii:ii + 1], in1=bx, op=ALU.mult))
                    dv(lambda h: h.tensor_tensor(out=t_, in0=sa, in1=sb, op=ALU.mult))
                    dv(lambda h: h.scalar_tensor_tensor(out=cq, in0=ca, scalar=cb, in1=t_, op0=ALU.mult, op1=ALU.subtract))
                    dv(lambda h: h.tensor_tensor(out=t_, in0=ca, in1=sb, op=ALU.mult))
                    dv(lambda h: h.scalar_tensor_tensor(out=sq, in0=sa, scalar=cb, in1=t_, op0=ALU.mult, op1=ALU.add))
                    dv(lambda h: h.tensor_scalar(out=tmp[:, 0:32], in0=Rs[:, 0:32], scalar1=sq, scalar2=None, op0=ALU.mult))
                    dv(lambda h: h.scalar_tensor_tensor(out=RowC, in0=Rc[:, 0:32], scalar=cq, in1=tmp[:, 0:32],
                                                        op0=ALU.mult, op1=ALU.subtract))
                    dv(lambda h: h.tensor_scalar(out=tmp[:, 0:32], in0=Rc[:, 0:32], scalar1=sq, scalar2=None, op0=ALU.mult))
                    dv(lambda h: h.scalar_tensor_tensor(out=RowS, in0=Rs[:, 0:32], scalar=cq, in1=tmp[:, 0:32],
                                                        op0=ALU.mult, op1=ALU.add))
                    rc_, rs_ = RowC, RowS
                row_b = lambda x: x.unsqueeze(2).broadcast_to([128, 32, 64])
                col_b = lambda x: x.unsqueeze(1).broadcast_to([128, 32, 64])
                dv(lambda h: h.tensor_scalar(out=Tc3, in0=row_b(rc_), scalar1=isrow, scalar2=None, op0=ALU.mult))
                dv(lambda h: h.scalar_tensor_tensor(out=Tc3, in0=col_b(Rc[:, 0:64]), scalar=nsel, in1=Tc3,
                                                    op0=ALU.mult, op1=ALU.add))
                dv(lambda h: h.tensor_scalar(out=Ts3, in0=row_b(rs_), scalar1=rsg, scalar2=None, op0=ALU.mult))
                dv(lambda h: h.scalar_tensor_tensor(out=Ts3, in0=col_b(Rs[:, 0:64]), scalar=nsg, in1=Ts3,
                                                    op0=ALU.mult, op1=ALU.add))
                dst = tabs[sel, which].rearrange("tb p (k t) -> p tb k t", k=2)
                dma("sp", dst[:, :, 0, :], Tc.rearrange("p (tb t) -> p tb t", tb=NTB), r=[R_s], w=[R_tabs])
                dma("sp", dst[:, :, 1, :], Ts.rearrange("p (tb t) -> p tb t", tb=NTB), r=[R_s], w=[R_tabs])

    build_tables()

    def vcol(l, c, n=1):
        return vec_sb[:, l, c:c + n]

    class Rot:
        def __init__(self, name, n, shape, dt=BF16):
            self.bufs = [ar.alloc(shape, dt) for _ in range(n)]
            self.regs = [Region(f"{name}{i}") for i in range(n)]
            self.i = 0

        def next(self):
            k = self.i % len(self.bufs)
            self.i += 1
            return self.bufs[k], self.regs[k]

    class Stream:
        def __init__(self, name, n, shape, srcs, deps, ahead, q=None, dt=BF16):
            self.bufs = [ar.alloc(shape, dt) for _ in range(n)]
            self.regs = [Region(f"{name}{i}") for i in range(n)]
            self.srcs, self.deps, self.ahead, self.q = srcs, deps, ahead, (q or WQ)
            self.issued = 0
            self.taken = 0

        def _issue_upto(self, k):
            while self.issued <= min(k, len(self.srcs) - 1):
                i = self.issued
                b = i % len(self.bufs)
                dma(self.q, self.bufs[b], self.srcs[i], r=self.deps, w=[self.regs[b]])
                self.issued += 1

        def get(self):
            i = self.taken
            self.taken += 1
            self._issue_upto(i + self.ahead)
            b = i % len(self.bufs)
            return self.bufs[b], self.regs[b]

    def rstd_from_ss(ss_bank, R_ss, n_feat, out, R_out):
        op("act", lambda h: h.activation(out=out, in_=ss_bank, func=AF.Ln, bias=epsc[:, 0:1],
                                         scale=1.0 / n_feat), r=[R_ss, R_const], w=[R_out])
        op("act", lambda h: h.activation(out=out, in_=out, func=AF.Exp, scale=-0.5),
           r=[R_out], w=[R_out])

    def norm_x_to_h(l, gcol, hT, R_h, tmp):
        sq, R_sq, rs, R_rs, sbank, R_sb = tmp
        for tb in range(NTB):
            ts = slice(tb * TB, (tb + 1) * TB)
            op("act", lambda h: h.activation(out=sq, in_=xT[:, :, ts], func=AF.Square),
               r=[R_x[tb]], w=[R_sq])
            for kc in range(8):
                op("pe", lambda h: h.matmul(sbank, lhsT=ones_bf, rhs=sq[:, kc, :],
                                            start=(kc == 0), stop=(kc == 7)),
                   r=[R_sq, R_const], w=[R_sb])
            rstd_from_ss(sbank, R_sb, D, rs, R_rs)
            for kc in range(8):
                op("dve", lambda h: h.scalar_tensor_tensor(
                    out=hT[:, kc, ts], in0=xT[:, kc, ts], scalar=vcol(l, gcol + kc), in1=rs,
                    op0=ALU.mult, op1=ALU.mult), r=[R_x[tb], R_rs, R_vec], w=[R_h[tb]])

    for u in unit_list:
        is_sample = (u == NUNITS - 1)
        nchunks = 4 if is_sample else 1
        tabsel = 1 if is_sample else 0
        ar.reset(base_mark)
        tk.barrier()
        WQ = "sp" if is_sample else "act"
        dma(WQ, xT, xin[u].rearrange("p (k t) -> p k t", k=8), w=R_x, semreg=R_x[0])

        for l in range(n_layers if unit_layers is None else unit_layers[u]):
            ar.reset(base_mark)
            tk.barrier()
            ar.off = (ar.off + 31) // 32 * 32
            attn_off = ar.mark()
            attnA = ar.alloc([128, 4, T], BF16)
            attnB = ar.alloc([128, 4, T], BF16)
            R_attnA = [Region(f"attnA{tb}") for tb in range(NTB)]
            R_attnB = [Region(f"attnB{tb}") for tb in range(NTB)]
            attn_mark = ar.mark()
            qA = ar.alloc([128, 4, T], BF16)
            R_qA = [Region(f"qA{tb}") for tb in range(NTB)]
            cqn = ar.alloc([128, 3, T], BF16)
            R_cqn = [Region(f"cqn{tb}") for tb in range(NTB)]
            krT = ar.alloc([128, T], BF16)
            R_kr = Region("krT")
            nbuf = 1
            kA = [ar.alloc([128, 2, T], BF16) for _ in range(nbuf)]
            vA = [ar.alloc([128, 16, 193], BF16) for _ in range(nbuf)]
            R_kvA = [Region(f"kvA{i}") for i in range(nbuf)]
            ckvn = [ar.alloc([128, 2, T], BF16) for _ in range(nbuf)]
            R_ckvn = [Region(f"ckvn{i}") for i in range(nbuf)]
            p1_mark = ar.mark()
            hT = ar.alloc([128, 8, T], BF16, at=attn_off)
            R_h = [Region(f"h{tb}") for tb in range(NTB)]
            tabS = [ar.alloc([128, 2, TB], F32) for _ in range(2)]
            R_tabS = [Region(f"tabS{i}") for i in range(2)]
            tab_i = [0]

            def load_tab(which, tb, rows=slice(0, 128)):
                import os
                if os.environ.get("KDBG_NOTAB") and tab_i[0] >= 2:
                    return tabS[0], R_tabS[0]
                k = tab_i[0] % 2
                tab_i[0] += 1
                dma(WQ, tabS[k][rows], tabs[tabsel, which, tb, rows, :].rearrange("p (k t) -> p k t", k=2),
                    r=[R_tabs], w=[R_tabS[k]])
                return tabS[k], R_tabS[k]
            sq = ar.alloc([128, 8, TB], BF16)
            R_sq = Region("sq")
            rs = ar.alloc([128, TB], F32)
            R_rs = Region("rs")
            P1_ORDER = [0, 4, 1, 5, 2, 6, 3, 7, 8, 9, 10, 11, 12, 13, 14, 15, 16, 17]
            wst1 = Stream("w1_", 6, [128, 8, 128], [wsrc("in", l, g, 8) for g in P1_ORDER], [R_w[("in", l)]], ahead=2)
            p1_pos = [0]
            NT1 = 1
            t1 = [ar.alloc([128, TB], F32) for _ in range(NT1)]
            t2 = [ar.alloc([128, TB], F32) for _ in range(NT1)]
            R_t = [Region(f"t{i}") for i in range(NT1)]
            sq2 = [ar.alloc([128, 3, TB], BF16) for _ in range(NT1)]
            R_sq2 = [Region(f"sq2{i}") for i in range(NT1)]
            rs2 = [ar.alloc([128, TB], F32) for _ in range(NT1)]
            R_rs2 = [Region(f"rs2{i}") for i in range(NT1)]
            cqf = [ar.alloc([128, 3, TB], F32) for _ in range(NT1)]
            R_cqf = [Region(f"cqf{i}") for i in range(NT1)]

            for i in range(nbuf):
                op("pool", lambda h: h.memset(kA[i][64:128, 0, :], 0.0), w=[R_kvA[i]])
                op("pool", lambda h: h.memset(kA[i][0:64, 1, :], 0.0), w=[R_kvA[i]])
                op("pool", lambda h: h.memset(vA[i][:, :, 64:65], 1.0), w=[R_kvA[i]])
                op("pool", lambda h: h.memset(vA[i][:, :, 65:129], 0.0), w=[R_kvA[i]])
                op("pool", lambda h: h.memset(vA[i][:, :, 65:66], 1.0), w=[R_kvA[i]])

            norm_x_to_h(l, V_GPRE, hT, R_h, (sq, R_sq, rs, R_rs, PB[7], R_PB[7]))

            pbi = [0]

            def next_banks(n):
                res = []
                for _ in range(n):
                    k = pbi[0] % 6
                    pbi[0] += 1
                    res.append(k)
                return res
            sbi = [0]

            def next_sbank():
                k = 6 + (sbi[0] % 2)
                sbi[0] += 1
                return k

            def load_groups(gs):
                res = []
                for g in gs:
                    assert P1_ORDER[p1_pos[0]] == g
                    p1_pos[0] += 1
                    res.append(wst1.get())
                return res

            def proj(wt, R_wt, tb, bank, M=128):
                ts = slice(tb * TB, (tb + 1) * TB)
                for kc in range(8):
                    op("pe", lambda h: h.matmul(PB[bank][0:M, :], lhsT=wt[:, kc, 0:M], rhs=hT[:, kc, ts],
                                                start=(kc == 0), stop=(kc == 7)),
                       r=[R_wt, R_h[tb]], w=[R_PB[bank]])

            it = [0]

            def qk_epilogue(tb, bq, bs, gcol, out_ap, R_out):
                ts = slice(tb * TB, (tb + 1) * TB)
                i = it[0] % NT1
                it[0] += 1
                tabA, R_tab = load_tab(0, tb)
                op("act", lambda h: h.activation(out=sq2[i][:, 0, :], in_=PB[bq], func=AF.Square),
                   r=[R_PB[bq]], w=[R_sq2[i]])
                sb = next_sbank()
                op("pe", lambda h: h.matmul(PB[sb], lhsT=bd_bf, rhs=sq2[i][:, 0, :], start=True, stop=True),
                   r=[R_sq2[i], R_const], w=[R_PB[sb]])
                rstd_from_ss(PB[sb], R_PB[sb], 64, rs2[i], R_rs2[i])
                op("dve", lambda h: h.scalar_tensor_tensor(
                    out=t1[i], in0=PB[bq], scalar=vcol(l, gcol), in1=tabA[:, 0, :],
                    op0=ALU.mult, op1=ALU.mult), r=[R_PB[bq], R_tab, R_vec], w=[R_t[i]])
                op("dve", lambda h: h.scalar_tensor_tensor(
                    out=t2[i], in0=PB[bs], scalar=vcol(l, gcol + 1), in1=tabA[:, 1, :],
                    op0=ALU.mult, op1=ALU.mult), r=[R_PB[bs], R_tab, R_vec], w=[R_t[i]])
                op("dve", lambda h: h.tensor_tensor(out=t1[i], in0=t1[i], in1=t2[i], op=ALU.add),
                   r=[R_t[i]], w=[R_t[i]])
                if isinstance(out_ap, list):
                    for (rows_, oap) in out_ap:
                        op("dve", lambda h: h.tensor_tensor(out=oap, in0=t1[i][rows_, :], in1=rs2[i][rows_, :], op=ALU.mult),
                           r=[R_t[i], R_rs2[i]], w=[R_out])
                else:
                    op("dve", lambda h: h.tensor_tensor(out=out_ap, in0=t1[i], in1=rs2[i], op=ALU.mult),
                       r=[R_t[i], R_rs2[i]], w=[R_out])

            for j in range(4):
                (wq, R_wq), (ws, R_ws) = load_groups([j, 4 + j])
                for tb in range(NTB):
                    ts = slice(tb * TB, (tb + 1) * TB)
                    bq, bs = next_banks(2)
                    proj(wq, R_wq, tb, bq)
                    proj(ws, R_ws, tb, bs)
                    qk_epilogue(tb, bq, bs, V_GQ, qA[:, j, ts], R_qA[tb])
            (wq, R_wq), (ws, R_ws) = load_groups([8, 9])
            for tb in range(NTB):
                ts = slice(tb * TB, (tb + 1) * TB)
                bq, bs = next_banks(2)
                proj(wq, R_wq, tb, bq)
                proj(ws, R_ws, tb, bs)
                qk_epilogue(tb, bq, bs, V_GK, [(slice(0, 64), kA[0][0:64, 0, ts]), (slice(64, 128), kA[0][64:128, 1, ts])],
                            R_kvA[0])
            (wv, R_wv), = load_groups([10])
            for tg in range(4):
                (bv,) = next_banks(1)
                for tt in range(4):
                    tok = slice((tg * 4 + tt) * 128, (tg * 4 + tt + 1) * 128)
                    for kc in range(8):
                        op("pe", lambda h: h.matmul(PB[bv][:, tt * 128:(tt + 1) * 128], lhsT=hT[:, kc, tok],
                                                    rhs=wv[:, kc, :], start=(kc == 0), stop=(kc == 7)),
                           r=[R_wv, R_h[tg]], w=[R_PB[bv]])
                src = PB[bv].rearrange("p (t k d) -> p t k d", t=4, k=2)
                op("act", lambda h: h.activation(out=vA[0][:, tg * 4:(tg + 1) * 4, 0:64], in_=src[:, :, 0, :],
                                                 func=AF.Identity), r=[R_PB[bv]], w=[R_kvA[0]])
                op("act", lambda h: h.activation(out=vA[0][:, tg * 4:(tg + 1) * 4, 129:193], in_=src[:, :, 1, :],
                                                 func=AF.Identity), r=[R_PB[bv]], w=[R_kvA[0]])

            def latent(groups, gcol, n_feat, out_fn, R_out_fn):
                ws_ = load_groups(groups)
                ng = len(groups)
                for tb in range(NTB):
                    ts = slice(tb * TB, (tb + 1) * TB)
                    i = it[0] % NT1
                    it[0] += 1
                    banks = next_banks(ng)
                    for gi in range(ng):
                        proj(ws_[gi][0], ws_[gi][1], tb, banks[gi])
                    sb = next_sbank()
                    for gi in range(ng):
                        b = banks[gi]
                        op("act", lambda h: h.activation(out=sq2[i][:, gi, :], in_=PB[b], func=AF.Square),
                           r=[R_PB[b]], w=[R_sq2[i]])
                        op("act", lambda h: h.activation(out=cqf[i][:, gi, :], in_=PB[b], func=AF.Identity),
                           r=[R_PB[b]], w=[R_cqf[i]])
                        op("pe", lambda h: h.matmul(PB[sb], lhsT=ones_bf, rhs=sq2[i][:, gi, :],
                                                    start=(gi == 0), stop=(gi == ng - 1)),
                           r=[R_sq2[i], R_const], w=[R_PB[sb]])
                    rstd_from_ss(PB[sb], R_PB[sb], n_feat, rs2[i], R_rs2[i])
                    for gi in range(ng):
                        op("dve", lambda h: h.scalar_tensor_tensor(
                            out=out_fn(gi, ts), in0=cqf[i][:, gi, :], scalar=vcol(l, gcol + gi), in1=rs2[i],
                            op0=ALU.mult, op1=ALU.mult), r=[R_cqf[i], R_rs2[i], R_vec], w=[R_out_fn(tb)])

            latent([11, 12, 13], V_GCQ, 384, lambda gi, ts: cqn[:, gi, ts], lambda tb: R_cqn[tb])
            latent([14, 15], V_GCKV, 256, lambda gi, ts: ckvn[0][:, gi, ts], lambda tb: R_ckvn[0])

            (wq, R_wq), (ws, R_ws) = load_groups([16, 17])
            for tb in range(NTB):
                ts = slice(tb * TB, (tb + 1) * TB)
                i = it[0] % NT1
                it[0] += 1
                bq, bs = next_banks(2)
                proj(wq, R_wq, tb, bq, M=96)
                proj(ws, R_ws, tb, bs, M=96)
                tabB, R_tab = load_tab(1, tb, rows=slice(64, 96))
                op("dve", lambda h: h.tensor_tensor(out=t1[i][64:96, :], in0=PB[bq][64:96, :],
                                                    in1=tabB[64:96, 0, :], op=ALU.mult),
                   r=[R_PB[bq], R_tab], w=[R_t[i]])
                op("dve", lambda h: h.tensor_tensor(out=t2[i][64:96, :], in0=PB[bs][64:96, :],
                                                    in1=tabB[64:96, 1, :], op=ALU.mult),
                   r=[R_PB[bs], R_tab], w=[R_t[i]])
                op("dve", lambda h: h.tensor_tensor(out=krT[64:96, ts], in0=t1[i][64:96, :],
                                                    in1=t2[i][64:96, :], op=ALU.add),
                   r=[R_t[i]], w=[R_kr])

            if is_sample:
                g_ = gin[l]
                dma("pool", g_[0][0:64, :], kA[0][0:64, 0, :], r=[R_kvA[0]], w=[R_gin[l][0]])
                dma("pool", g_[0][64:128, :], kA[0][64:128, 1, :], r=[R_kvA[0]], w=[R_gin[l][0]])
                dma("pool", g_[3].rearrange("p (t c) -> p t c", t=16), vA[0], r=[R_kvA[0]], w=[R_gin[l][3]])
                dma("pool", g_[1].rearrange("(c p) t -> p c t", p=128), ckvn[0],
                    r=[R_ckvn[0]], w=[R_gin[l][1]])
                dma("pool", g_[2], krT[64:96, :], r=[R_kr], w=[R_gin[l][2]])
                for p_ in range(NGP):
                    tk.custom("pool", lambda h: h.collective_compute(
                        "AllGather", ALU.bypass, replica_groups=[[0, 1, 2, 3], [4, 5, 6, 7]],
                        ins=[gin[l][p_].opt()], outs=[gout[l][p_].opt()]), cc_sems[l][p_],
                        r=[R_gin[l][p_]], w=[R_gout[l][p_]])

            ar.reset(p1_mark)
            tk.barrier()
            tabQ = [ar.alloc([128, 2, TB], F32) for _ in range(1)]
            R_tabQ = [Region(f"tabQ{i}") for i in range(1)]
            NPT = 5
            pT = [ar.alloc([128, TB], BF16) for _ in range(NPT)]
            R_pT = [Region(f"pT{i}") for i in range(NPT)]
            NT2 = 1
            rrow = [ar.alloc([128, TB], F32) for _ in range(NT2)]
            R_rrow = [Region(f"rrow{i}") for i in range(NT2)]
            rb = [ar.alloc([128, TB], F32) for _ in range(NT2)]
            R_rb = [Region(f"rb{i}") for i in range(NT2)]
            Kh = [ar.alloc([128, T], BF16) for _ in range(2)]
            Vh = [ar.alloc([128, 16, 192], BF16) for _ in range(2)]
            R_KVh = [Region(f"KVh{i}") for i in range(2)]
            Qh = [ar.alloc([128, T], BF16) for _ in range(2)]
            R_Qh = [Region(f"Qh{i}") for i in range(2)]
            wq_st = Stream("wuq", 2, [128, 2, 3, 96],
                           [WB["uq"][l, h_ * 128:(h_ + 1) * 128, :].rearrange("p (v k m) -> p v k m", v=2, k=3) for h_ in range(8)],
                           [R_w[("uq", l)]], ahead=1)
            wkv_st = Stream("wukv", 2, [128, 2, 128], [wsrc("ukv", l, h_, 2) for h_ in range(8)],
                            [R_w[("ukv", l)]], ahead=1)
            tq = [ar.alloc([128, TB], F32) for _ in range(2)]
            R_tq = [Region(f"tq{i}") for i in range(2)]
            for i in range(2):
                op("pool", lambda h: h.memset(Kh[i][96:128, :], 0.0), w=[R_KVh[i]])
                op("pool", lambda h: h.memset(Qh[i][96:128, :], 0.0), w=[R_Qh[i]])
                op("pool", lambda h: h.memset(Vh[i][:, :, 0:64], 0.0), w=[R_KVh[i]])
                op("pool", lambda h: h.memset(Vh[i][:, :, 0:1], 1.0), w=[R_KVh[i]])
                op("pool", lambda h: h.memset(Vh[i][:, :, 128:192], 0.0), w=[R_KVh[i]])
                op("pool", lambda h: h.memset(Vh[i][:, :, 128:129], 1.0), w=[R_KVh[i]])

            OB = [0, 1, 2, 3]
            SB = [4, 5, 6, 7]
            DEPTH = 3
            BB = 6
            GB = 7
            GB2 = 6
            s_i = [0]
            n_i = [0]

            def attention_core(q_fn, R_q, k_fn, v_fn, R_kv, hi, scale, c, last):
                steps = [(qb, kt) for qb in range(NTB) for kt in range(16 if not os.environ.get("KDBG_NOATT") else 1)]
                M = 128 if hi else 65
                queue = []
                for st_ in steps + [None] * DEPTH:
                    if st_ is not None:
                        qb, kt = st_
                        si = s_i[0]
                        s_i[0] += 1
                        sb_ = SB[si % len(SB)]
                        pi = si % NPT
                        qs = slice(qb * TB, (qb + 1) * TB)
                        op("pe", lambda h: h.matmul(PB[sb_], lhsT=k_fn(kt), rhs=q_fn(qs), start=True, stop=True),
                           r=[R_kv, R_q[qb]], w=[R_PB[sb_]])
                        op("act", lambda h: h.activation(out=pT[pi], in_=PB[sb_], func=AF.Exp, bias=0.0, scale=scale),
                           r=[R_PB[sb_]], w=[R_pT[pi]])
                        queue.append((qb, kt, pi))
                    if queue and (len(queue) > DEPTH or st_ is None):
                        pqb, pkt, ppi = queue.pop(0)
                        op("pe", lambda h: h.matmul(PB[OB[pqb]][0:M, :], lhsT=v_fn(pkt), rhs=pT[ppi],
                                                    start=(c == 0 and pkt == 0), stop=(last and pkt == 15)),
                           r=[R_kv, R_pT[ppi]], w=[R_PB[OB[pqb]]])

            def normalize(hi, out_fn, R_out_fn):
                for qb in range(NTB):
                    qs = slice(qb * TB, (qb + 1) * TB)
                    i = n_i[0] % NT2
                    n_i[0] += 1
                    ob = OB[qb]
                    if not hi:
                        op("act", lambda h: h.activation(out=rrow[i][64:65, :], in_=PB[ob][64:65, :], func=AF.Ln),
                           r=[R_PB[ob]], w=[R_rrow[i]])
                        op("act", lambda h: h.activation(out=rrow[i][64:65, :], in_=rrow[i][64:65, :], func=AF.Exp,
                                                         scale=-1.0), r=[R_rrow[i]], w=[R_rrow[i]])
                        op("pe", lambda h: h.matmul(PB[BB][0:64, :], lhsT=ones_lo[64:65, 0:64], rhs=rrow[i][64:65, :],
                                                    start=True, stop=True), r=[R_rrow[i], R_const], w=[R_PB[BB]])
                        rows = slice(0, 64)
                    else:
                        op("act", lambda h: h.activation(out=rrow[i][0:1, :], in_=PB[ob][0:1, :], func=AF.Ln),
                           r=[R_PB[ob]], w=[R_rrow[i]])
                        op("act", lambda h: h.activation(out=rrow[i][0:1, :], in_=rrow[i][0:1, :], func=AF.Exp,
                                                         scale=-1.0), r=[R_rrow[i]], w=[R_rrow[i]])
                        op("pe", lambda h: h.matmul(PB[BB], lhsT=ones_hi[0:1, :], rhs=rrow[i][0:1, :],
                                                    start=True, stop=True), r=[R_rrow[i], R_const], w=[R_PB[BB]])
                        rows = slice(64, 128)
                    op("dve", lambda h: h.tensor_copy(out=rb[i][rows, :], in_=PB[BB][rows, :]),
                       r=[R_PB[BB]], w=[R_rb[i]])
                    op("dve", lambda h: h.tensor_tensor(out=out_fn(rows, qs), in0=PB[ob][rows, :], in1=rb[i][rows, :],
                                                        op=ALU.mult), r=[R_PB[ob], R_rb[i]], w=[R_out_fn(qb)])

            ld_i = [0]
            for hA in range(8):
                kv = hA // 4
                j = hA % 4
                rows = slice(kv * 64, kv * 64 + 64)
                for c in range(nchunks):
                    if is_sample:
                        bi = 0
                        dma("sp", kA[bi][0:64, 0, :], gout[l][0][c * 128:c * 128 + 64, :], r=[R_gout[l][0]], w=[R_kvA[bi]])
                        dma("sp", kA[bi][64:128, 1, :], gout[l][0][c * 128 + 64:(c + 1) * 128, :], r=[R_gout[l][0]],
                            w=[R_kvA[bi]])
                        dma("sp", vA[bi], gout[l][3][c * 128:(c + 1) * 128, :].rearrange("p (t c) -> p t c", t=16),
                            r=[R_gout[l][3]], w=[R_kvA[bi]])
                    else:
                        bi = 0
                    if kv == 0:
                        v_fn = (lambda kt, bi=bi: vA[bi][:, kt, 0:128])
                    else:
                        v_fn = (lambda kt, bi=bi: vA[bi][:, kt, 65:193])
                    attention_core(
                        q_fn=lambda qs, j=j: qA[:, j, qs], R_q=R_qA,
                        k_fn=lambda kt, bi=bi, kv=kv: kA[bi][:, kv, kt * 128:(kt + 1) * 128],
                        v_fn=v_fn, R_kv=R_kvA[bi], hi=True, scale=SCALE_A,
                        c=c, last=(c == nchunks - 1))
                normalize(kv == 1, lambda rows_, qs, j=j: attnA[rows_, j, qs], lambda qb: R_attnA[qb])

            if not is_sample:
                for i in range(2):
                    op("pool", lambda h: h.tensor_copy(out=Kh[i][64:96, :], in_=krT[64:96, :]),
                       r=[R_kr], w=[R_KVh[i]])
            g_i = [0]
            for hB in range(8):
                hi = (hB % 2 == 1)
                rows = slice(64, 128) if hi else slice(0, 64)
                wq_t, R_wq_t = wq_st.get()
                wkv_t, R_wkv_t = wkv_st.get()
                qi = hB % 2
                for qb in range(NTB):
                    qs = slice(qb * TB, (qb + 1) * TB)
                    tqi = 0
                    g_i[0] += 1
                    dma(WQ, tabQ[tqi][64:96], tabs[tabsel, 1, qb, 64:96, :].rearrange("p (k t) -> p k t", k=2),
                        r=[R_tabs], w=[R_tabQ[tqi]])
                    for kc in range(3):
                        op("pe", lambda h: h.matmul(PB[GB][0:96, :], lhsT=wq_t[:, 0, kc, :], rhs=cqn[:, kc, qs],
                                                    start=(kc == 0), stop=(kc == 2)),
                           r=[R_wq_t, R_cqn[qb]], w=[R_PB[GB]])
                    op("dve", lambda h: h.tensor_copy(out=Qh[qi][0:64, qs], in_=PB[GB][0:64, :]),
                       r=[R_PB[GB]], w=[R_Qh[qi]])
                    op("dve", lambda h: h.tensor_tensor(out=tq[0][64:96, :], in0=PB[GB][64:96, :],
                                                        in1=tabQ[tqi][64:96, 0, :], op=ALU.mult),
                       r=[R_PB[GB], R_tabQ[tqi]], w=[R_tq[0]])
                    for kc in range(3):
                        op("pe", lambda h: h.matmul(PB[GB2][0:96, :], lhsT=wq_t[:, 1, kc, :], rhs=cqn[:, kc, qs],
                                                    start=(kc == 0), stop=(kc == 2)),
                           r=[R_wq_t, R_cqn[qb]], w=[R_PB[GB2]])
                    op("dve", lambda h: h.tensor_tensor(out=tq[1][64:96, :], in0=PB[GB2][64:96, :],
                                                        in1=tabQ[tqi][64:96, 1, :], op=ALU.mult),
                       r=[R_PB[GB2], R_tabQ[tqi]], w=[R_tq[1]])
                    op("dve", lambda h: h.tensor_tensor(out=Qh[qi][64:96, qs], in0=tq[0][64:96, :],
                                                        in1=tq[1][64:96, :], op=ALU.add),
                       r=[R_tq[0], R_tq[1]], w=[R_Qh[qi]])
                for c in range(nchunks):
                    ki = g_i[0] % 2
                    g_i[0] += 1
                    if is_sample:
                        bi = 0
                        dma("sp", ckvn[bi], gout[l][1][c * 256:(c + 1) * 256, :].rearrange("(c p) t -> p c t", p=128),
                            r=[R_gout[l][1]], w=[R_ckvn[bi]])
                        dma("sp", Kh[ki][64:96, :], gout[l][2][c * 32:(c + 1) * 32, :], r=[R_gout[l][2]], w=[R_KVh[ki]])
                    else:
                        bi = 0
                    for kb in range(4):
                        GBK = GB if kb % 2 == 0 else GB2
                        ks = slice(kb * TB, (kb + 1) * TB)
                        for kc in range(2):
                            op("pe", lambda h: h.matmul(PB[GBK][0:64, :], lhsT=wkv_t[:, kc, 0:64], rhs=ckvn[bi][:, kc, ks],
                                                        start=(kc == 0), stop=(kc == 1)),
                               r=[R_wkv_t, R_ckvn[bi]], w=[R_PB[GBK]])
                        op("dve", lambda h: h.tensor_copy(out=Kh[ki][0:64, ks], in_=PB[GBK][0:64, :]),
                           r=[R_PB[GBK]], w=[R_KVh[ki]])
                    for tg in range(2):
                        GBV = GB if tg % 2 == 0 else GB2
                        for tt in range(8):
                            kt = tg * 8 + tt
                            for kc in range(2):
                                op("pe", lambda h: h.matmul(PB[GBV][:, tt * 64:(tt + 1) * 64],
                                                            lhsT=ckvn[bi][:, kc, kt * 128:(kt + 1) * 128],
                                                            rhs=wkv_t[:, kc, 64:128], start=(kc == 0), stop=(kc == 1)),
                                   r=[R_wkv_t, R_ckvn[bi]], w=[R_PB[GBV]])
                        src = PB[GBV].rearrange("p (t d) -> p t d", t=8)
                        dst = Vh[ki][:, tg * 8:(tg + 1) * 8, 64:128]
                        op("dve", lambda h: h.tensor_copy(out=dst, in_=src),
                           r=[R_PB[GBV]], w=[R_KVh[ki]])
                    if hi:
                        v_fn = (lambda kt, ki=ki: Vh[ki][:, kt, 0:128])
                    else:
                        v_fn = (lambda kt, ki=ki: Vh[ki][:, kt, 64:192])
                    attention_core(
                        q_fn=lambda qs, qi=qi: Qh[qi][:, qs], R_q=[R_Qh[qi]] * NTB,
                        k_fn=lambda kt, ki=ki: Kh[ki][:, kt * 128:(kt + 1) * 128],
                        v_fn=v_fn, R_kv=R_KVh[ki], hi=True, scale=SCALE_B,
                        c=c, last=(c == nchunks - 1))
                normalize(hi, lambda rows_, qs, jj=hB // 2: attnB[rows_, jj, qs], lambda qb: R_attnB[qb])

            ar.reset(attn_mark)
            tk.barrier()
            mT = ar.alloc([128, 8, T], BF16)
            R_m = [Region(f"m{tb}") for tb in range(NTB)]
            p3_mark = ar.mark()
            hT = ar.alloc([128, 8, T], BF16)
            R_h = [Region(f"h{tb}") for tb in range(NTB)]
            sq = ar.alloc([128, 8, TB], BF16)
            R_sq = Region("sq")
            rs = ar.alloc([128, TB], F32)
            R_rs = Region("rs")
            norm_x_to_h(l, V_GPRE, hT, R_h, (sq, R_sq, rs, R_rs, PB[7], R_PB[7]))
            woa_st = Stream("woa", 2, [128, 4, 128], [wsrc("oa", l, o_, 4) for o_ in range(8)], [R_w[("oa", l)]], ahead=1)
            wob_st = Stream("wob", 2, [128, 4, 128], [wsrc("ob", l, o_, 4) for o_ in range(8)], [R_w[("ob", l)]], ahead=1)
            wg_st = Stream("wg", 4, [128, 8, 128],
                           [wsrc("in", l, (18 if k_ == 0 else 26) + o_, 8) for o_ in range(8) for k_ in range(2)],
                           [R_w[("in", l)]], ahead=2)
            ga = [ar.alloc([128, TB], F32) for _ in range(2)]
            gb = [ar.alloc([128, TB], F32) for _ in range(2)]
            R_g = [Region(f"g{i}") for i in range(2)]
            e_i = 0
            for o in range(8):
                woa_t, R_woa = woa_st.get()
                wob_t, R_wob = wob_st.get()
                wga_t, R_wga = wg_st.get()
                wgb_t, R_wgb = wg_st.get()
                for tb in range(NTB):
                    ts = slice(tb * TB, (tb + 1) * TB)
                    i = e_i % 2
                    b0 = (e_i % 2) * 4
                    e_i += 1
                    for j in range(4):
                        op("pe", lambda h: h.matmul(PB[b0], lhsT=woa_t[:, j, :], rhs=attnA[:, j, ts],
                                                    start=(j == 0), stop=(j == 3)),
                           r=[R_woa, R_attnA[tb]], w=[R_PB[b0]])
                    for j in range(4):
                        op("pe", lambda h: h.matmul(PB[b0 + 1], lhsT=wob_t[:, j, :], rhs=attnB[:, j, ts],
                                                    start=(j == 0), stop=(j == 3)),
                           r=[R_wob, R_attnB[tb]], w=[R_PB[b0 + 1]])
                    for kc in range(8):
                        op("pe", lambda h: h.matmul(PB[b0 + 2], lhsT=wga_t[:, kc, :], rhs=hT[:, kc, ts],
                                                    start=(kc == 0), stop=(kc == 7)),
                           r=[R_wga, R_h[tb]], w=[R_PB[b0 + 2]])
                    for kc in range(8):
                        op("pe", lambda h: h.matmul(PB[b0 + 3], lhsT=wgb_t[:, kc, :], rhs=hT[:, kc, ts],
                                                    start=(kc == 0), stop=(kc == 7)),
                           r=[R_wgb, R_h[tb]], w=[R_PB[b0 + 3]])
                    op("act", lambda h: h.activation(out=ga[i], in_=PB[b0 + 2], func=AF.Sigmoid,
                                                     bias=vcol(l, V_BG + o), scale=1.0),
                       r=[R_PB[b0 + 2], R_vec], w=[R_g[i]])
                    op("act", lambda h: h.activation(out=gb[i], in_=PB[b0 + 3], func=AF.Sigmoid,
                                                     bias=vcol(l, V_BG + 8 + o), scale=1.0),
                       r=[R_PB[b0 + 3], R_vec], w=[R_g[i]])
                    op("dve", lambda h: h.tensor_tensor(out=ga[i], in0=ga[i], in1=PB[b0], op=ALU.mult),
                       r=[R_g[i], R_PB[b0]], w=[R_g[i]])
                    op("dve", lambda h: h.tensor_tensor(out=gb[i], in0=gb[i], in1=PB[b0 + 1], op=ALU.mult),
                       r=[R_g[i], R_PB[b0 + 1]], w=[R_g[i]])
                    op("dve", lambda h: h.tensor_tensor(out=mT[:, o, ts], in0=ga[i], in1=gb[i], op=ALU.add),
                       r=[R_g[i]], w=[R_m[tb]])

            ar.reset(p3_mark)
            tk.barrier()
            wout_t = ar.alloc([128, 8, 8, 128], BF16)
            R_wout = Region("wout")
            for o in range(8):
                dma(WQ, wout_t[:, o], wsrc("out", l, o, 8), r=[R_w[("out", l)]], w=[R_wout])
            mix = [ar.alloc([128, 8, TB], F32) for _ in range(2)]
            R_mix = [Region(f"mix{i}") for i in range(2)]
            sq3 = [ar.alloc([128, 8, TB], BF16) for _ in range(2)]
            R_sq3 = [Region(f"sq3{i}") for i in range(2)]
            rs3 = [ar.alloc([128, TB], F32) for _ in range(2)]
            R_rs3 = [Region(f"rs3{i}") for i in range(2)]
            tt3 = [ar.alloc([128, TB], F32) for _ in range(2)]
            R_tt3 = [Region(f"tt3{i}") for i in range(2)]
            bk = 0
            for tb in range(NTB):
                ts = slice(tb * TB, (tb + 1) * TB)
                i = tb % 2
                sbk = 6 + (tb % 2)
                for o in range(8):
                    b = bk % 6
                    bk += 1
                    for kc in range(8):
                        op("pe", lambda h: h.matmul(PB[b], lhsT=wout_t[:, o, kc, :], rhs=mT[:, kc, ts],
                                                    start=(kc == 0), stop=(kc == 7)),
                           r=[R_wout, R_m[tb]], w=[R_PB[b]])
                    op("act", lambda h: h.activation(out=mix[i][:, o, :], in_=PB[b], func=AF.Identity),
                       r=[R_PB[b]], w=[R_mix[i]])
                    op("act", lambda h: h.activation(out=sq3[i][:, o, :], in_=PB[b], func=AF.Square),
                       r=[R_PB[b]], w=[R_sq3[i]])
                    op("pe", lambda h: h.matmul(PB[sbk], lhsT=ones_bf, rhs=sq3[i][:, o, :],
                                                start=(o == 0), stop=(o == 7)),
                       r=[R_sq3[i], R_const], w=[R_PB[sbk]])
                rstd_from_ss(PB[sbk], R_PB[sbk], D, rs3[i], R_rs3[i])
                for o in range(8):
                    op("dve", lambda h: h.scalar_tensor_tensor(
                        out=tt3[i], in0=mix[i][:, o, :], scalar=vcol(l, V_GPOST + o), in1=rs3[i],
                        op0=ALU.mult, op1=ALU.mult), r=[R_mix[i], R_rs3[i], R_vec], w=[R_tt3[i]])
                    op("dve", lambda h: h.tensor_tensor(out=xT[:, o, ts], in0=xT[:, o, ts], in1=tt3[i], op=ALU.add),
                       r=[R_tt3[i], R_x[tb]], w=[R_x[tb]])

            ar.reset(base_mark)
            tk.barrier()
            hT = ar.alloc([128, 8, T], BF16)
            R_h = [Region(f"h{tb}") for tb in range(NTB)]
            sq = ar.alloc([128, 8, TB], BF16)
            R_sq = Region("sq")
            rs = ar.alloc([128, TB], F32)
            R_rs = Region("rs")
            norm_x_to_h(l, V_GFPRE, hT, R_h, (sq, R_sq, rs, R_rs, PB[7], R_PB[7]))
            if is_sample:
                op("dve", lambda h: h.tensor_copy(out=hx[:, 0:8], in_=hT[:, :, 0]), r=[R_h[0]], w=[R_hx])
                op("dve", lambda h: h.tensor_copy(out=hx[:, 8:16], in_=hT[:, :, T - 1]), r=[R_h[3]], w=[R_hx])
                dma("pool", hgin[l], hx, r=[R_hx], w=[R_hgin[l]])
                tk.custom("pool", lambda h: h.collective_compute(
                    "AllGather", ALU.bypass, replica_groups=[[0, 1, 2, 3], [4, 5, 6, 7]],
                    ins=[hgin[l].opt()], outs=[hgout[l].opt()]), cc_h[l], r=[R_hgin[l]], w=[R_hgout[l]])
                dma("sp", hg, hgout[l].rearrange("(r p) c -> p r c", p=128), r=[R_hgout[l]], w=[R_hg])
                for side in range(2):
                    cs = slice(8, 16) if side == 0 else slice(0, 8)
                    op("dve", lambda h: h.tensor_scalar(out=hacc[:, side, :], in0=hg[:, 0, cs],
                                                        scalar1=nbr_sb[:, 4 * side:4 * side + 1], scalar2=None,
                                                        op0=ALU.mult), r=[R_hg, R_vec], w=[R_halo])
                    for r_ in range(1, 4):
                        op("dve", lambda h: h.scalar_tensor_tensor(
                            out=hacc[:, side, :], in0=hg[:, r_, cs], scalar=nbr_sb[:, 4 * side + r_:4 * side + r_ + 1],
                            in1=hacc[:, side, :], op0=ALU.mult, op1=ALU.add), r=[R_hg, R_vec, R_halo], w=[R_halo])
                    op("dve", lambda h: h.tensor_copy(out=halo_b[:, side, :], in_=hacc[:, side, :]),
                       r=[R_halo], w=[R_halo])
            HT_ = 1024
            gbuf = ar.alloc([128, NCH, HT_], BF16)
            R_gbuf = Region("gbuf")
            p4_mark = ar.mark()
            for hf in range(2):
                ar.reset(p4_mark)
                if hf == 1:
                    tk.barrier()
                t0 = hf * HT_
                wup_st = Stream("wup", 4, [128, 8, 128],
                                [wsrc("up", l, c_ + w2 * NCH, 8) for c_ in range(NCH) for w2 in range(2)],
                                [R_w[("up", l)]], ahead=2)
                abuf = [[ar.alloc([128, HT_ + 2], F32) for _ in range(2)] for _ in range(2)]
                R_ab = [[Region(f"ab{w_}{i}") for i in range(2)] for w_ in range(2)]
                ubuf = [[ar.alloc([128, HT_], F32) for _ in range(2)] for _ in range(2)]
                R_ub = [[Region(f"ub{w_}{i}") for i in range(2)] for w_ in range(2)]
                out_col = 0 if hf == 0 else HT_ + 1
                if not is_sample:
                    for w_ in range(2):
                        for i in range(2):
                            op("pool", lambda h: h.memset(abuf[w_][i][:, out_col:out_col + 1], 0.0), w=[R_ab[w_][i]])
                halo_tok = (t0 + HT_) if hf == 0 else (t0 - 1)
                halo_col = (HT_ + 1) if hf == 0 else 0
                halo_tb = halo_tok // TB
                bk = 0
                for c in range(NCH):
                    par = c % 2
                    for w_ in range(2):
                        wt, R_wt = wup_st.get()
                        g = c + w_ * NCH
                        a_ = abuf[w_][par]
                        for blk in range(2):
                            tbi = (t0 // TB) + blk
                            ts = slice(t0 + blk * TB, t0 + (blk + 1) * TB)
                            b = bk % 6
                            bk += 1
                            for kc in range(8):
                                op("pe", lambda h: h.matmul(PB[b], lhsT=wt[:, kc, :], rhs=hT[:, kc, ts],
                                                            start=(kc == 0), stop=(kc == 7)),
                                   r=[R_wt, R_h[tbi]], w=[R_PB[b]])
                            op("act", lambda h: h.activation(out=a_[:, 1 + blk * TB:1 + (blk + 1) * TB], in_=PB[b],
                                                             func=AF.Identity), r=[R_PB[b]], w=[R_ab[w_][par]])
                        hb = 6 + (bk % 2)
                        for kc in range(8):
                            op("pe", lambda h: h.matmul(PB[hb][:, 0:1], lhsT=wt[:, kc, :],
                                                        rhs=hT[:, kc, halo_tok:halo_tok + 1],
                                                        start=(kc == 0), stop=(kc == 7)),
                               r=[R_wt, R_h[halo_tb]], w=[R_PB[hb]])
                        op("act", lambda h: h.activation(out=a_[:, halo_col:halo_col + 1], in_=PB[hb][:, 0:1],
                                                         func=AF.Identity), r=[R_PB[hb]], w=[R_ab[w_][par]])
                        if is_sample:
                            for kc in range(8):
                                op("pe", lambda h: h.matmul(PB[hb][:, 8:9], lhsT=wt[:, kc, :],
                                                            rhs=halo_b[:, hf, kc:kc + 1],
                                                            start=(kc == 0), stop=(kc == 7)),
                                   r=[R_wt, R_halo], w=[R_PB[hb]])
                            op("act", lambda h: h.activation(out=a_[:, out_col:out_col + 1], in_=PB[hb][:, 8:9],
                                                             func=AF.Identity), r=[R_PB[hb]], w=[R_ab[w_][par]])
                        u_ = ubuf[w_][par]
                        cw = lambda jj: vcol(l, V_CW + jj * 2 * NCH + g)
                        op("act", lambda h: h.activation(out=u_, in_=a_[:, 1:HT_ + 1], func=AF.Identity,
                                                         bias=vcol(l, V_CB + g), scale=cw(1)),
                           r=[R_ab[w_][par], R_vec], w=[R_ub[w_][par]])
                        op("dve", lambda h: h.scalar_tensor_tensor(
                            out=u_, in0=a_[:, 0:HT_], scalar=cw(0), in1=u_, op0=ALU.mult, op1=ALU.add),
                           r=[R_ab[w_][par], R_ub[w_][par], R_vec], w=[R_ub[w_][par]])
                        op("dve", lambda h: h.scalar_tensor_tensor(
                            out=u_, in0=a_[:, 2:HT_ + 2], scalar=cw(2), in1=u_, op0=ALU.mult, op1=ALU.add),
                           r=[R_ab[w_][par], R_ub[w_][par], R_vec], w=[R_ub[w_][par]])
                    ug, uv = ubuf[0][par], ubuf[1][par]
                    op("act", lambda h: h.activation(out=ug, in_=ug, func=AF.Gelu_apprx_tanh),
                       r=[R_ub[0][par]], w=[R_ub[0][par]])
                    op("dve", lambda h: h.tensor_tensor(out=gbuf[:, c, :], in0=ug, in1=uv, op=ALU.mult),
                       r=[R_ub[0][par], R_ub[1][par]], w=[R_gbuf])
                ar.reset(p4_mark)
                tk.barrier()
                wd_st = Stream("wd", 2, [128, NCH, 128], [wsrc("down", l, o_, NCH) for _b in range(2) for o_ in range(8)],
                               [R_w[("down", l)]], ahead=1)
                fsb = ar.alloc([128, 8, TB], F32)
                R_fsb = Region("fsb")
                sq4 = [ar.alloc([128, TB], BF16) for _ in range(2)]
                R_sq4 = [Region(f"sq4{i}") for i in range(2)]
                rs4 = ar.alloc([128, TB], F32)
                R_rs4 = Region("rs4")
                tt4 = ar.alloc([128, TB], F32)
                R_tt4 = Region("tt4")
                bk = 0
                q_i = 0
                for blk in range(2):
                    bs_ = slice(blk * TB, (blk + 1) * TB)
                    tbi = (t0 // TB) + blk
                    ts = slice(t0 + blk * TB, t0 + (blk + 1) * TB)
                    sbk = 6 + blk
                    for o in range(8):
                        wd_t, R_wd = wd_st.get()
                        b = bk % 6
                        bk += 1
                        for c in range(NCH):
                            op("pe", lambda h: h.matmul(PB[b], lhsT=wd_t[:, c, :], rhs=gbuf[:, c, bs_],
                                                        start=(c == 0), stop=(c == NCH - 1)),
                               r=[R_wd, R_gbuf], w=[R_PB[b]])
                        op("act", lambda h: h.activation(out=fsb[:, o, :], in_=PB[b], func=AF.Identity),
                           r=[R_PB[b]], w=[R_fsb])
                        qi_ = q_i % 2
                        q_i += 1
                        op("act", lambda h: h.activation(out=sq4[qi_], in_=PB[b], func=AF.Square),
                           r=[R_PB[b]], w=[R_sq4[qi_]])
                        op("pe", lambda h: h.matmul(PB[sbk], lhsT=ones_bf, rhs=sq4[qi_],
                                                    start=(o == 0), stop=(o == 7)),
                           r=[R_sq4[qi_], R_const], w=[R_PB[sbk]])
                    rstd_from_ss(PB[sbk], R_PB[sbk], D, rs4, R_rs4)
                    for o in range(8):
                        op("dve", lambda h: h.scalar_tensor_tensor(
                            out=tt4, in0=fsb[:, o, :], scalar=vcol(l, V_GFPOST + o), in1=rs4,
                            op0=ALU.mult, op1=ALU.mult), r=[R_fsb, R_rs4, R_vec], w=[R_tt4])
                        op("dve", lambda h: h.tensor_tensor(out=xT[:, o, ts], in0=xT[:, o, ts], in1=tt4,
                                                            op=ALU.add), r=[R_tt4, R_x[tbi]], w=[R_x[tbi]])

        dma(WQ, yout[u].rearrange("p (k t) -> p k t", k=8), xT, r=R_x, w=[R_yout])

    tk.barrier()
    return nc, tk


def _rope_tables():
    def table(pos_rows, pos_cols, dim):
        half = dim // 2
        nf = half // 2
        inv = (10000.0 ** (-np.arange(0, half, 2, dtype=np.float32) / half)).astype(np.float32)
        ang = np.concatenate([pos_rows[:, None] * inv, pos_cols[:, None] * inv], axis=-1).astype(np.float32)
        cos = np.cos(ang).astype(np.float32)
        sin = np.sin(ang).astype(np.float32)
        p = np.arange(128)
        d = p % dim
        pair = d // 2
        sign = np.where(d % 2 == 0, -1.0, 1.0).astype(np.float32)
        c = cos[:, pair].T
        s = (sin[:, pair].T * sign[:, None])
        return np.ascontiguousarray(c, dtype=np.float32), np.ascontiguousarray(s, dtype=np.float32)
    return table


def _prep_weights(inp):
    sw = lambda d: d ^ 1
    d64 = np.arange(64)
    d32 = np.arange(32)
    cols = []
    for j in range(4):
        cols.append(np.concatenate([j * 64 + d64, (j + 4) * 64 + d64]))
    for j in range(4):
        cols.append(np.concatenate([j * 64 + sw(d64), (j + 4) * 64 + sw(d64)]))
    cols.append(512 + np.concatenate([d64, 64 + d64]))
    cols.append(512 + np.concatenate([sw(d64), 64 + sw(d64)]))
    cols.append(640 + np.arange(128))
    for g in range(3):
        cols.append(768 + g * 128 + np.arange(128))
    for g in range(2):
        cols.append(1152 + g * 128 + np.arange(128))
    cols.append(np.concatenate([1152 + d64, 1408 + d32, 1408 + d32]))
    cols.append(np.concatenate([1152 + d64, 1408 + sw(d32), 1408 + sw(d32)]))
    for g in range(16):
        cols.append(1440 + g * 128 + np.arange(128))
    cols = np.stack(cols)
    w_in = inp["w_in"]
    wi = w_in[:, :, cols]
    wi = wi.reshape(L, 8, 128, NG_IN, 128).transpose(0, 3, 2, 1, 4)
    out = {"w_in": np.ascontiguousarray(wi).reshape(L, NG_IN * 128, 1024)}
    wq = inp["w_uq"].reshape(L, 3, 128, 8, 96)
    idx_sw = np.concatenate([np.arange(64), 64 + sw(d32)])
    wq2 = np.stack([wq, wq[..., idx_sw]], axis=0)
    out["w_uq"] = np.ascontiguousarray(wq2.transpose(1, 4, 3, 0, 2, 5)).reshape(L, 8 * 128, 576)
    wkv = inp["w_ukv"].reshape(L, 2, 128, 8, 128)
    out["w_ukv"] = np.ascontiguousarray(wkv.transpose(0, 3, 2, 1, 4)).reshape(L, 1024, 256)
    woa = inp["w_oa"].reshape(L, 2, 4, 64, 8, 128)
    out["w_oa"] = np.ascontiguousarray(woa.transpose(0, 4, 1, 3, 2, 5)).reshape(L, 1024, 512)
    wob = inp["w_ob"].reshape(L, 4, 128, 8, 128)
    out["w_ob"] = np.ascontiguousarray(wob.transpose(0, 3, 2, 1, 4)).reshape(L, 1024, 512)
    wo = inp["w_out"].reshape(L, 8, 128, 8, 128)
    out["w_out"] = np.ascontiguousarray(wo.transpose(0, 3, 2, 1, 4)).reshape(L, 1024, 1024)
    wu = inp["w_up"].reshape(L, 8, 128, 2 * NCH, 128)
    out["w_up"] = np.ascontiguousarray(wu.transpose(0, 3, 2, 1, 4)).reshape(L, 2 * NCH * 128, 1024)
    wd = inp["w_down"].reshape(L, NCH, 128, 8, 128)
    out["w_down"] = np.ascontiguousarray(wd.transpose(0, 3, 2, 1, 4)).reshape(L, 1024, NCH * 128)
    vec = np.zeros((L, 128, NV), np.float32)
    p = np.arange(128)
    for l in range(L):
        vec[l, :, V_GPRE:V_GPRE + 8] = inp["g_mix_pre"][l].reshape(8, 128).T
        vec[l, :, V_GQ] = inp["g_qa"][l][p % 64]
        vec[l, :, V_GQS] = inp["g_qa"][l][(p % 64) ^ 1]
        vec[l, :, V_GK] = inp["g_ka"][l][p % 64]
        vec[l, :, V_GKS] = inp["g_ka"][l][(p % 64) ^ 1]
        vec[l, :, V_GCQ:V_GCQ + 3] = inp["g_cq"][l].reshape(3, 128).T
        vec[l, :, V_GCKV:V_GCKV + 2] = inp["g_ckv"][l].reshape(2, 128).T
        vec[l, :, V_BG:V_BG + 16] = inp["b_gates"][l].reshape(16, 128).T
        vec[l, :, V_GPOST:V_GPOST + 8] = inp["g_mix_post"][l].reshape(8, 128).T
        vec[l, :, V_GFPRE:V_GFPRE + 8] = inp["g_ffn_pre"][l].reshape(8, 128).T
        vec[l, :, V_CW:V_CW + 132] = inp["conv_w"][l].reshape(3, 2 * NCH, 128).transpose(2, 0, 1).reshape(128, 132)
        vec[l, :, V_CB:V_CB + 44] = inp["conv_b"][l].reshape(2 * NCH, 128).T
        vec[l, :, V_GFPOST:V_GFPOST + 8] = inp["g_ffn_post"][l].reshape(8, 128).T
    out["vecs"] = vec
    return out


_CACHE = {}


def _pidx(q):
    p = np.arange(128)
    out = np.zeros((128, 8), np.float32)
    for which, (dim, nf) in enumerate(((64, 16), (32, 8))):
        d = p % dim
        pair = d // 2
        out[:, 3 * which] = pair % nf
        out[:, 3 * which + 1] = (pair < nf)
        out[:, 3 * which + 2] = np.where(d % 2 == 0, -1.0, 1.0)
    out[:, 6] = q & 1
    out[:, 7] = q >> 1
    return out


def _to_dev(x):
    return x.T.reshape(8, 128, T).transpose(1, 0, 2).reshape(128, 8 * T)


def _from_dev(y):
    return y.reshape(128, 8, T).transpose(1, 0, 2).reshape(D, T).T


def make_tabs(table, rowp, colp, q):
    t4 = np.empty((2, 4, 128, T), np.float32)
    t4[0, 0], t4[0, 1] = table(rowp, colp, 64)
    t4[0, 2], t4[0, 3] = table(rowp, colp, 32)
    t4[1, 0], t4[1, 1] = table(rowp + 32.0 * q, colp, 64)
    t4[1, 2], t4[1, 3] = table(rowp + 32.0 * q, colp, 32)
    t6 = t4.reshape(2, 2, 2, 128, NTB, TB)
    return np.ascontiguousarray(t6.transpose(0, 1, 4, 3, 2, 5)).reshape(2, 2, NTB, 128, 2 * TB)


def kernel(**inputs):
    inp = {k: np.asarray(v) for k, v in inputs.items()}
    xp = inp["x_prompt"]
    xs = inp["x_sample"]
    wts = _prep_weights(inp)
    table = _rope_tables()
    j = np.arange(T, dtype=np.float32)
    colp = np.mod(j, 64.0).astype(np.float32)
    rowp = np.floor(j / 64.0).astype(np.float32)
    in_maps = []
    for c in range(NCORES):
        xin = np.empty((NUNITS, 128, 8 * T), np.float32)
        for u in range(4):
            xin[u] = _to_dev(xp[4 * c + u])
        s, q = c // 4, c % 4
        xin[4] = _to_dev(xs[s, q * T:(q + 1) * T])
        nb = np.zeros((128, 8), np.float32)
        if q - 1 >= 0:
            nb[:, q - 1] = 1.0
        if q + 1 <= 3:
            nb[:, 4 + q + 1] = 1.0
        m = {"xin": xin, "pidx": _pidx(q), "nbr": nb}
        m.update(wts)
        in_maps.append(m)
    if "nc" not in _CACHE:
        _CACHE["nc"] = build_program()[0]
    nc = _CACHE["nc"]
    res = run_bass_kernel_spmd(nc, in_maps, core_ids=list(range(NCORES)))
    yp = np.empty_like(xp)
    ys = np.empty_like(xs)
    for c in range(NCORES):
        y = res.results[c]["yout"]
        for u in range(4):
            yp[4 * c + u] = _from_dev(y[u])
        s, q = c // 4, c % 4
        ys[s, q * T:(q + 1) * T] = _from_dev(y[4])
    return (yp, ys)
```

```python
import os
import numpy as np
import concourse.bass as bass
import concourse.mybir as mybir
from concourse.bass_utils import run_bass_kernel_spmd

F32 = mybir.dt.float32
BF16 = mybir.dt.bfloat16
U8 = mybir.dt.uint8
AF = mybir.ActivationFunctionType
ALU = mybir.AluOpType

NCORES = 8
D = 1024
T = 2048
NTB = 4
TB = 512
NUNITS = 5
L = 2
EPS = 1e-6
DFF = 2816
NCH = 22
NG_IN = 34
NV = 233
GROWS = 544
SCALE_A = 64 ** -0.5
SCALE_B = 96 ** -0.5

V_GPRE, V_GQ, V_GQS, V_GK, V_GKS, V_GCQ, V_GCKV, V_BG, V_GPOST, V_GFPRE, V_CW, V_CB, V_GFPOST = (
    0, 8, 9, 10, 11, 12, 15, 17, 33, 41, 49, 181, 225)


_REGS = {}


def Region(name):
    if name not in _REGS:
        _REGS[name] = _Region(name)
    return _REGS[name]


class _Region:
    __slots__ = ("name", "w", "r", "sem", "persist")

    def __init__(self, name):
        self.name = name
        self.w = None
        self.r = {}
        self.sem = None
        self.persist = False


class Eng:
    def __init__(self, name, h, sem, is_pe=False):
        self.name = name
        self.h = h
        self.sem = sem
        self.cnt = 0
        self.seen = {}
        self.is_pe = is_pe


class Tracker:
    def __init__(self, nc):
        self.nc = nc
        self._sems = []
        self.E = {}
        for name, h in (("pe", nc.tensor), ("act", nc.scalar), ("dve", nc.vector),
                        ("pool", nc.gpsimd), ("sp", nc.sync)):
            self.E[name] = Eng(name, h, self.new_sem("e_" + name), is_pe=(name == "pe"))
        self.touched = set()
        self.n_inst = 0
        self.n_wait = 0
        self.pool = []
        self.pool_used = 0
        self.semcnt = {}
        self.pooled_regs = []
        self.outq = {}
        self.max_out = int(os.environ.get("KDBG_MAXOUT", "2"))

    def new_sem(self, name):
        g = self.nc.semaphore(name)
        s = g.__enter__()
        self._sems.append(g)
        return s

    def _wait(self, e, deps):
        for sem, val in deps.items():
            if e.seen.get(sem, 0) < val:
                e.h.wait_ge(sem, val)
                e.seen[sem] = val
                self.n_wait += 1

    def _deps(self, e, r, w):
        deps = {}

        def add(ev):
            if ev is None:
                return
            s, v = ev
            if e.is_pe and s is e.sem:
                return
            if deps.get(s, 0) < v:
                deps[s] = v
        for reg in r:
            add(reg.w)
        for reg in w:
            add(reg.w)
            for s, v in reg.r.items():
                add((s, v))
        return deps

    def _commit(self, ev, r, w):
        s, v = ev
        for reg in w:
            reg.w = ev
            reg.r = {}
            self.touched.add(reg)
        for reg in r:
            if reg.r.get(s, 0) < v:
                reg.r[s] = v
            self.touched.add(reg)

    def op(self, eng, fn, r=(), w=()):
        e = self.E[eng]
        self._wait(e, self._deps(e, r, w))
        ins = fn(e.h)
        e.cnt += 1
        ins.then_inc(e.sem, 1)
        self.n_inst += 1
        self._commit((e.sem, e.cnt), r, w)

    def dma(self, eng, out, in_, r=(), w=(), semreg=None):
        e = self.E[eng]
        self._wait(e, self._deps(e, r, w))
        reg = semreg if semreg is not None else w[0]
        if reg.sem is None:
            if reg.persist:
                reg.sem = self.new_sem("d_" + reg.name)
            else:
                if self.pool_used == len(self.pool):
                    self.pool.append(self.new_sem(f"dp{len(self.pool)}"))
                reg.sem = self.pool[self.pool_used]
                self.pool_used += 1
                self.pooled_regs.append(reg)
        fifo = self.outq.setdefault(eng, [])
        while len(fifo) >= self.max_out:
            s_, v_ = fifo.pop(0)
            self._wait(e, {s_: v_})
        ins = e.h.dma_start(out=out, in_=in_)
        v = self.semcnt.get(reg.sem, 0) + 16
        self.semcnt[reg.sem] = v
        ins.then_inc(reg.sem, 16)
        self.n_inst += 1
        fifo.append((reg.sem, v))
        self._commit((reg.sem, v), r, w)

    def custom(self, eng, fn, sem, r=(), w=()):
        e = self.E[eng]
        self._wait(e, self._deps(e, r, w))
        ins = fn(e.h)
        ins.then_inc(sem)
        self._commit((sem, 1), r, w)

    def barrier(self):
        deps = {}
        keep = set()
        for reg in self.touched:
            if reg.persist:
                keep.add(reg)
                continue
            evs = list(reg.r.items())
            if reg.w is not None:
                evs.append(reg.w)
            for s, v in evs:
                if deps.get(s, 0) < v:
                    deps[s] = v
            reg.w = None
            reg.r = {}
        self.touched = keep
        for e in self.E.values():
            d = {s: v for s, v in deps.items() if s is not e.sem}
            self._wait(e, d)
        for reg in self.pooled_regs:
            reg.sem = None
        self.pooled_regs = []
        self.pool_used = 0


class Arena:
    def __init__(self, nc, nbytes):
        self.t = nc.alloc_sbuf_tensor("arena", [128, nbytes], U8).ap()
        self.nbytes = nbytes
        self.off = 0

    def mark(self):
        return self.off

    def reset(self, m):
        self.off = m

    def alloc(self, shape, dt, at=None):
        esz = 4 if dt == F32 else 2
        n = int(np.prod(shape[1:])) * esz
        if at is None:
            self.off = (self.off + 31) // 32 * 32
            assert self.off + n <= self.nbytes, ("arena overflow", self.off, n, self.nbytes)
            o = self.off
            self.off += n
        else:
            o = at
        ap = self.t[:, o:o + n].bitcast(dt)
        if len(shape) == 3:
            ap = ap.rearrange("p (a b) -> p a b", a=shape[1])
        elif len(shape) == 4:
            ap = ap.rearrange("p (a b c) -> p a b c", a=shape[1], b=shape[2])
        return ap


def build_program(n_units=NUNITS, n_layers=L, units=None, unit_layers=None):
    nc = bass.Bass("TRN2", target_bir_lowering=False)
    _REGS.clear()
    tk = Tracker(nc)
    op, dma = tk.op, tk.dma
    import os
    WQ = os.environ.get("KDBG_WQ", "act")
    unit_list = list(range(n_units)) if units is None else units

    def din(name, shape, dt=F32):
        return nc.dram_tensor(name, list(shape), dt, kind="ExternalInput").ap()

    xin = din("xin", [NUNITS, 128, 8 * T])
    pidx = din("pidx", [128, 8])
    tabs = nc.dram_tensor("tabs_d", [2, 2, NTB, 128, 2 * TB], F32).ap()
    R_tabs = Region("tabs_d")
    R_tabs.persist = True
    vecs = din("vecs", [L, 128, NV])
    nbr = din("nbr", [128, 8])
    WSH = {"in": (NG_IN * 128, 1024), "uq": (8 * 128, 576), "ukv": (8 * 128, 256), "oa": (8 * 128, 512),
           "ob": (8 * 128, 512), "out": (8 * 128, 1024), "up": (2 * NCH * 128, 1024), "down": (8 * 128, NCH * 128)}
    WF = {k: din("w_" + k, [L, r, c]) for k, (r, c) in WSH.items()}
    yout = nc.dram_tensor("yout", [NUNITS, 128, 8 * T], F32, kind="ExternalOutput").ap()

    def dscr(name, shape, dt=BF16):
        return nc.dram_tensor(name, list(shape), dt).ap()

    WB = {k: dscr("b_" + k, [L, r, c]) for k, (r, c) in WSH.items()}

    def wsrc(name, l, g, k):
        return WB[name][l, g * 128:(g + 1) * 128, :].rearrange("p (k m) -> p k m", k=k)
    GP = [(128, T), (256, T), (32, T), (128, 16 * 193)]
    NGP = len(GP)
    gin = [[dscr(f"gin{l}_{p}", [GP[p][0], GP[p][1]]) for p in range(NGP)] for l in range(L)]
    gout = [[dscr(f"gout{l}_{p}", [4 * GP[p][0], GP[p][1]]) for p in range(NGP)] for l in range(L)]
    R_gin = [[Region(f"gin{l}_{p}") for p in range(NGP)] for l in range(L)]
    R_gout = [[Region(f"gout{l}_{p}") for p in range(NGP)] for l in range(L)]
    cc_sems = [[tk.new_sem(f"cc{l}_{p}") for p in range(NGP)] for l in range(L)]
    R_yout = Region("yout")
    hgin = [dscr(f"hgin{l}", [128, 16]) for l in range(L)]
    hgout = [dscr(f"hgout{l}", [512, 16]) for l in range(L)]
    R_hgin = [Region(f"hgin{l}") for l in range(L)]
    R_hgout = [Region(f"hgout{l}") for l in range(L)]
    cc_h = [tk.new_sem(f"cch{l}") for l in range(L)]

    R_w = {}

    def convert(name, l, rows_per):
        reg = Region(f"cv_{name}{l}")
        reg.persist = True
        R_w[(name, l)] = reg
        n0 = WSH[name][0]
        for i in range(0, n0, rows_per):
            j = min(n0, i + rows_per)
            dma("pool", WB[name][l, i:j, :], WF[name][l, i:j, :], w=[reg])

    for l in range(n_layers):
        convert("in", l, 512)
        convert("uq", l, 512)
        convert("ukv", l, 1024)
        convert("oa", l, 1024)
        convert("ob", l, 1024)
        convert("out", l, 512)
        convert("up", l, 512)
        convert("down", l, 256)

    PB = [nc.alloc_psum_tensor(f"pb{i}", [128, 512], F32).ap() for i in range(8)]
    R_PB = [Region(f"pb{i}") for i in range(8)]

    ar = Arena(nc, 204 * 1024)
    ones_bf = ar.alloc([128, 128], BF16)
    bd_bf = ar.alloc([128, 128], BF16)
    ones_lo = ar.alloc([128, 64], F32)
    ones_hi = ar.alloc([128, 128], F32)
    epsc = ar.alloc([128, 1], F32)
    vec_sb = ar.alloc([128, L, NV], F32)
    nbr_sb = ar.alloc([128, 8], F32)
    hx = ar.alloc([128, 16], BF16)
    hg = ar.alloc([128, 4, 16], BF16)
    hacc = ar.alloc([128, 2, 8], F32)
    halo_b = ar.alloc([128, 2, 8], BF16)
    R_hx = Region("hx")
    R_hg = Region("hg")
    R_halo = Region("halo")
    R_const = Region("const")
    R_vec = Region("vecs")
    xT = ar.alloc([128, 8, T], F32)
    R_x = [Region(f"x{tb}") for tb in range(NTB)]
    base_mark = ar.mark()

    op("dve", lambda h: h.memset(ones_bf, 1.0), w=[R_const])
    op("dve", lambda h: h.memset(bd_bf, 0.0), w=[R_const])
    op("dve", lambda h: h.memset(bd_bf[0:64, 0:64], 1.0), w=[R_const])
    op("dve", lambda h: h.memset(bd_bf[64:128, 64:128], 1.0), w=[R_const])
    op("dve", lambda h: h.memset(ones_lo, 1.0), w=[R_const])
    op("dve", lambda h: h.memset(ones_hi, 0.0), w=[R_const])
    op("dve", lambda h: h.memset(ones_hi[:, 64:128], 1.0), w=[R_const])
    op("dve", lambda h: h.memset(epsc, EPS), w=[R_const])
    dma("sp", vec_sb, vecs.rearrange("l p v -> p l v"), w=[R_vec])
    dma("sp", nbr_sb, nbr, w=[R_vec])

    for _i in range(int(os.environ.get("KDBG_PEPAD", "0"))):
        op("pe", lambda h: h.matmul(PB[7][:, 0:1], lhsT=ones_bf, rhs=ones_bf[:, 0:1], start=True, stop=True),
           r=[R_const], w=[R_PB[7]])


    def build_tables():
        ar.reset(base_mark)
        R_s = Region("setup")
        pid = ar.alloc([128, 8], F32)
        dma("sp", pid, pidx, w=[R_s])
        C = ar.alloc([128, 8], F32)
        S = ar.alloc([128, 8], F32)
        cl = ar.alloc([128, 32], F32)
        Rc = ar.alloc([128, 128], F32)
        Rs = ar.alloc([128, 128], F32)
        tmp = ar.alloc([128, 128], F32)
        RowC = ar.alloc([128, 32], F32)
        RowS = ar.alloc([128, 32], F32)
        Tc = ar.alloc([128, T], F32)
        Ts = ar.alloc([128, T], F32)
        Tc3 = Tc.rearrange("p (a b) -> p a b", a=32)
        Ts3 = Ts.rearrange("p (a b) -> p a b", a=32)

        def dv(fn):
            op("dve", fn, r=[R_s], w=[R_s])
        HALFPI, ONE = cl[:, 30:31], cl[:, 31:32]
        dv(lambda h: h.memset(HALFPI, float(np.pi / 2)))
        dv(lambda h: h.memset(ONE, 1.0))
        for which in range(2):
            NF = 16 if which == 0 else 8
            kcol, isrow, sign = pid[:, 3 * which:3 * which + 1], pid[:, 3 * which + 1:3 * which + 2], pid[:, 3 * which + 2:3 * which + 3]
            invf = cl[:, 0:1]
            op("act", lambda h: h.activation(out=invf, in_=kcol, func=AF.Exp, scale=float(-np.log(10000.0) / NF)),
               r=[R_s], w=[R_s])
            op("act", lambda h: h.activation(out=S[:, 0:1], in_=invf, func=AF.Sin), r=[R_s], w=[R_s])
            op("act", lambda h: h.activation(out=C[:, 0:1], in_=invf, func=AF.Sin, bias=HALFPI, scale=1.0),
               r=[R_s], w=[R_s])
            for i in range(1, 7):
                a_, b_ = slice(i - 1, i), slice(i, i + 1)
                dv(lambda h: h.tensor_tensor(out=cl[:, 1:2], in0=S[:, a_], in1=S[:, a_], op=ALU.mult))
                dv(lambda h: h.scalar_tensor_tensor(out=C[:, b_], in0=C[:, a_], scalar=C[:, a_], in1=cl[:, 1:2],
                                                    op0=ALU.mult, op1=ALU.subtract))
                dv(lambda h: h.tensor_scalar(out=S[:, b_], in0=S[:, a_], scalar1=C[:, a_], scalar2=2.0,
                                             op0=ALU.mult, op1=ALU.mult))
            dv(lambda h: h.memset(Rc[:, 0:1], 1.0))
            dv(lambda h: h.memset(Rs[:, 0:1], 0.0))
            for i in range(7):
                blk = 1 << i
                lo, hi_ = slice(0, blk), slice(blk, 2 * blk)
                ci, si = C[:, i:i + 1], S[:, i:i + 1]
                dv(lambda h: h.tensor_scalar(out=tmp[:, lo], in0=Rs[:, lo], scalar1=si, scalar2=None, op0=ALU.mult))
                dv(lambda h: h.scalar_tensor_tensor(out=Rc[:, hi_], in0=Rc[:, lo], scalar=ci, in1=tmp[:, lo],
                                                    op0=ALU.mult, op1=ALU.subtract))
                dv(lambda h: h.tensor_scalar(out=tmp[:, lo], in0=Rc[:, lo], scalar1=si, scalar2=None, op0=ALU.mult))
                dv(lambda h: h.scalar_tensor_tensor(out=Rs[:, hi_], in0=Rs[:, lo], scalar=ci, in1=tmp[:, lo],
                                                    op0=ALU.mult, op1=ALU.add))
            nsel, rsg, nsg = cl[:, 2:3], cl[:, 3:4], cl[:, 4:5]
            dv(lambda h: h.tensor_scalar(out=nsel, in0=isrow, scalar1=-1.0, scalar2=1.0, op0=ALU.mult, op1=ALU.add))
            dv(lambda h: h.tensor_tensor(out=rsg, in0=isrow, in1=sign, op=ALU.mult))
            dv(lambda h: h.tensor_tensor(out=nsg, in0=nsel, in1=sign, op=ALU.mult))
            for sel in range(2):
                if sel == 0:
                    rc_, rs_ = Rc[:, 0:32], Rs[:, 0:32]
                else:
                    b5, b6 = pid[:, 6:7], pid[:, 7:8]
                    ca, sa, cb, sb, cq, sq, t_ = (cl[:, 5 + j:6 + j] for j in range(7))
                    for (cx, sx, bx, ii) in ((ca, sa, b5, 5), (cb, sb, b6, 6)):
                        dv(lambda h: h.tensor_scalar(out=cx, in0=C[:, ii:ii + 1], scalar1=-1.0, scalar2=bx,
                                                     op0=ALU.add, op1=ALU.mult))
                        dv(lambda h: h.tensor_scalar(out=cx, in0=cx, scalar1=1.0, scalar2=None, op0=ALU.add))
                        dv(lambda h: h.tensor_tensor(out=sx, in0=S[:, ii:ii + 1], in1=bx, op=ALU.mult))
                    dv(lambda h: h.tensor_tensor(out=t_, in0=sa, in1=sb, op=ALU.mult))
                    dv(lambda h: h.scalar_tensor_tensor(out=cq, in0=ca, scalar=cb, in1=t_, op0=ALU.mult, op1=ALU.subtract))
                    dv(lambda h: h.tensor_tensor(out=t_, in0=ca, in1=sb, op=ALU.mult))
                    dv(lambda h: h.scalar_tensor_tensor(out=sq, in0=sa, scalar=cb, in1=t_, op0=ALU.mult, op1=ALU.add))
                    dv(lambda h: h.tensor_scalar(out=tmp[:, 0:32], in0=Rs[:, 0:32], scalar1=sq, scalar2=None, op0=ALU.mult))
                    dv(lambda h: h.scalar_tensor_tensor(out=RowC, in0=Rc[:, 0:32], scalar=cq, in1=tmp[:, 0:32],
                                                        op0=ALU.mult, op1=ALU.subtract))
                    dv(lambda h: h.tensor_scalar(out=tmp[:, 0:32], in0=Rc[:, 0:32], scalar1=sq, scalar2=None, op0=ALU.mult))
                    dv(lambda h: h.scalar_tensor_tensor(out=RowS, in0=Rs[:, 0:32], scalar=cq, in1=tmp[:, 0:32],
                                                        op0=ALU.mult, op1=ALU.add))
                    rc_, rs_ = RowC, RowS
                row_b = lambda x: x.unsqueeze(2).broadcast_to([128, 32, 64])
                col_b = lambda x: x.unsqueeze(1).broadcast_to([128, 32, 64])
                dv(lambda h: h.tensor_scalar(out=Tc3, in0=row_b(rc_), scalar1=isrow, scalar2=None, op0=ALU.mult))
                dv(lambda h: h.scalar_tensor_tensor(out=Tc3, in0=col_b(Rc[:, 0:64]), scalar=nsel, in1=Tc3,
                                                    op0=ALU.mult, op1=ALU.add))
                dv(lambda h: h.tensor_scalar(out=Ts3, in0=row_b(rs_), scalar1=rsg, scalar2=None, op0=ALU.mult))
                dv(lambda h: h.scalar_tensor_tensor(out=Ts3, in0=col_b(Rs[:, 0:64]), scalar=nsg, in1=Ts3,
                                                    op0=ALU.mult, op1=ALU.add))
                dst = tabs[sel, which].rearrange("tb p (k t) -> p tb k t", k=2)
                dma("sp", dst[:, :, 0, :], Tc.rearrange("p (tb t) -> p tb t", tb=NTB), r=[R_s], w=[R_tabs])
                dma("sp", dst[:, :, 1, :], Ts.rearrange("p (tb t) -> p tb t", tb=NTB), r=[R_s], w=[R_tabs])

    build_tables()

    def vcol(l, c, n=1):
        return vec_sb[:, l, c:c + n]

    class Rot:
        def __init__(self, name, n, shape, dt=BF16):
            self.bufs = [ar.alloc(shape, dt) for _ in range(n)]
            self.regs = [Region(f"{name}{i}") for i in range(n)]
            self.i = 0

        def next(self):
            k = self.i % len(self.bufs)
            self.i += 1
            return self.bufs[k], self.regs[k]

    class Stream:
        def __init__(self, name, n, shape, srcs, deps, ahead, q=None, dt=BF16):
            self.bufs = [ar.alloc(shape, dt) for _ in range(n)]
            self.regs = [Region(f"{name}{i}") for i in range(n)]
            self.srcs, self.deps, self.ahead, self.q = srcs, deps, ahead, (q or WQ)
            self.issued = 0
            self.taken = 0

        def _issue_upto(self, k):
            while self.issued <= min(k, len(self.srcs) - 1):
                i = self.issued
                b = i % len(self.bufs)
                dma(self.q, self.bufs[b], self.srcs[i], r=self.deps, w=[self.regs[b]])
                self.issued += 1

        def get(self):
            i = self.taken
            self.taken += 1
            self._issue_upto(i + self.ahead)
            b = i % len(self.bufs)
            return self.bufs[b], self.regs[b]

    def rstd_from_ss(ss_bank, R_ss, n_feat, out, R_out):
        op("act", lambda h: h.activation(out=out, in_=ss_bank, func=AF.Ln, bias=epsc[:, 0:1],
                                         scale=1.0 / n_feat), r=[R_ss, R_const], w=[R_out])
        op("act", lambda h: h.activation(out=out, in_=out, func=AF.Exp, scale=-0.5),
           r=[R_out], w=[R_out])

    def norm_x_to_h(l, gcol, hT, R_h, tmp):
        sq, R_sq, rs, R_rs, sbank, R_sb = tmp
        for tb in range(NTB):
            ts = slice(tb * TB, (tb + 1) * TB)
            op("act", lambda h: h.activation(out=sq, in_=xT[:, :, ts], func=AF.Square),
               r=[R_x[tb]], w=[R_sq])
            for kc in range(8):
                op("pe", lambda h: h.matmul(sbank, lhsT=ones_bf, rhs=sq[:, kc, :],
                                            start=(kc == 0), stop=(kc == 7)),
                   r=[R_sq, R_const], w=[R_sb])
            rstd_from_ss(sbank, R_sb, D, rs, R_rs)
            for kc in range(8):
                op("dve", lambda h: h.scalar_tensor_tensor(
                    out=hT[:, kc, ts], in0=xT[:, kc, ts], scalar=vcol(l, gcol + kc), in1=rs,
                    op0=ALU.mult, op1=ALU.mult), r=[R_x[tb], R_rs, R_vec], w=[R_h[tb]])

    for u in unit_list:
        is_sample = (u == NUNITS - 1)
        nchunks = 4 if is_sample else 1
        tabsel = 1 if is_sample else 0
        ar.reset(base_mark)
        tk.barrier()
        WQ = "sp" if is_sample else "act"
        dma(WQ, xT, xin[u].rearrange("p (k t) -> p k t", k=8), w=R_x, semreg=R_x[0])

        for l in range(n_layers if unit_layers is None else unit_layers[u]):
            ar.reset(base_mark)
            tk.barrier()
            ar.off = (ar.off + 31) // 32 * 32
            attn_off = ar.mark()
            attnA = ar.alloc([128, 4, T], BF16)
            attnB = ar.alloc([128, 4, T], BF16)
            R_attnA = [Region(f"attnA{tb}") for tb in range(NTB)]
            R_attnB = [Region(f"attnB{tb}") for tb in range(NTB)]
            attn_mark = ar.mark()
            qA = ar.alloc([128, 4, T], BF16)
            R_qA = [Region(f"qA{tb}") for tb in range(NTB)]
            cqn = ar.alloc([128, 3, T], BF16)
            R_cqn = [Region(f"cqn{tb}") for tb in range(NTB)]
            krT = ar.alloc([128, T], BF16)
            R_kr = Region("krT")
            nbuf = 1
            kA = [ar.alloc([128, 2, T], BF16) for _ in range(nbuf)]
            vA = [ar.alloc([128, 16, 193], BF16) for _ in range(nbuf)]
            R_kvA = [Region(f"kvA{i}") for i in range(nbuf)]
            ckvn = [ar.alloc([128, 2, T], BF16) for _ in range(nbuf)]
            R_ckvn = [Region(f"ckvn{i}") for i in range(nbuf)]
            p1_mark = ar.mark()
            hT = ar.alloc([128, 8, T], BF16, at=attn_off)
            R_h = [Region(f"h{tb}") for tb in range(NTB)]
            tabS = [ar.alloc([128, 2, TB], F32) for _ in range(2)]
            R_tabS = [Region(f"tabS{i}") for i in range(2)]
            tab_i = [0]

            def load_tab(which, tb, rows=slice(0, 128)):
                import os
                if os.environ.get("KDBG_NOTAB") and tab_i[0] >= 2:
                    return tabS[0], R_tabS[0]
                k = tab_i[0] % 2
                tab_i[0] += 1
                dma(WQ, tabS[k][rows], tabs[tabsel, which, tb, rows, :].rearrange("p (k t) -> p k t", k=2),
                    r=[R_tabs], w=[R_tabS[k]])
                return tabS[k], R_tabS[k]
            sq = ar.alloc([128, 8, TB], BF16)
            R_sq = Region("sq")
            rs = ar.alloc([128, TB], F32)
            R_rs = Region("rs")
            P1_ORDER = [0, 4, 1, 5, 2, 6, 3, 7, 8, 9, 10, 11, 12, 13, 14, 15, 16, 17]
            wst1 = Stream("w1_", 6, [128, 8, 128], [wsrc("in", l, g, 8) for g in P1_ORDER], [R_w[("in", l)]], ahead=2)
            p1_pos = [0]
            NT1 = 1
            t1 = [ar.alloc([128, TB], F32) for _ in range(NT1)]
            t2 = [ar.alloc([128, TB], F32) for _ in range(NT1)]
            R_t = [Region(f"t{i}") for i in range(NT1)]
            sq2 = [ar.alloc([128, 3, TB], BF16) for _ in range(NT1)]
            R_sq2 = [Region(f"sq2{i}") for i in range(NT1)]
            rs2 = [ar.alloc([128, TB], F32) for _ in range(NT1)]
            R_rs2 = [Region(f"rs2{i}") for i in range(NT1)]
            cqf = [ar.alloc([128, 3, TB], F32) for _ in range(NT1)]
            R_cqf = [Region(f"cqf{i}") for i in range(NT1)]

            for i in range(nbuf):
                op("pool", lambda h: h.memset(kA[i][64:128, 0, :], 0.0), w=[R_kvA[i]])
                op("pool", lambda h: h.memset(kA[i][0:64, 1, :], 0.0), w=[R_kvA[i]])
                op("pool", lambda h: h.memset(vA[i][:, :, 64:65], 1.0), w=[R_kvA[i]])
                op("pool", lambda h: h.memset(vA[i][:, :, 65:129], 0.0), w=[R_kvA[i]])
                op("pool", lambda h: h.memset(vA[i][:, :, 65:66], 1.0), w=[R_kvA[i]])

            norm_x_to_h(l, V_GPRE, hT, R_h, (sq, R_sq, rs, R_rs, PB[7], R_PB[7]))

            pbi = [0]

            def next_banks(n):
                res = []
                for _ in range(n):
                    k = pbi[0] % 6
                    pbi[0] += 1
                    res.append(k)
                return res
            sbi = [0]

            def next_sbank():
                k = 6 + (sbi[0] % 2)
                sbi[0] += 1
                return k

            def load_groups(gs):
                res = []
                for g in gs:
                    assert P1_ORDER[p1_pos[0]] == g
                    p1_pos[0] += 1
                    res.append(wst1.get())
                return res

            def proj(wt, R_wt, tb, bank, M=128):
                ts = slice(tb * TB, (tb + 1) * TB)
                for kc in range(8):
                    op("pe", lambda h: h.matmul(PB[bank][0:M, :], lhsT=wt[:, kc, 0:M], rhs=hT[:, kc, ts],
                                                start=(kc == 0), stop=(kc == 7)),
                       r=[R_wt, R_h[tb]], w=[R_PB[bank]])

            it = [0]

            def qk_epilogue(tb, bq, bs, gcol, out_ap, R_out):
                ts = slice(tb * TB, (tb + 1) * TB)
                i = it[0] % NT1
                it[0] += 1
                tabA, R_tab = load_tab(0, tb)
                op("act", lambda h: h.activation(out=sq2[i][:, 0, :], in_=PB[bq], func=AF.Square),
                   r=[R_PB[bq]], w=[R_sq2[i]])
                sb = next_sbank()
                op("pe", lambda h: h.matmul(PB[sb], lhsT=bd_bf, rhs=sq2[i][:, 0, :], start=True, stop=True),
                   r=[R_sq2[i], R_const], w=[R_PB[sb]])
                rstd_from_ss(PB[sb], R_PB[sb], 64, rs2[i], R_rs2[i])
                op("dve", lambda h: h.scalar_tensor_tensor(
                    out=t1[i], in0=PB[bq], scalar=vcol(l, gcol), in1=tabA[:, 0, :],
                    op0=ALU.mult, op1=ALU.mult), r=[R_PB[bq], R_tab, R_vec], w=[R_t[i]])
                op("dve", lambda h: h.scalar_tensor_tensor(
                    out=t2[i], in0=PB[bs], scalar=vcol(l, gcol + 1), in1=tabA[:, 1, :],
                    op0=ALU.mult, op1=ALU.mult), r=[R_PB[bs], R_tab, R_vec], w=[R_t[i]])
                op("dve", lambda h: h.tensor_tensor(out=t1[i], in0=t1[i], in1=t2[i], op=ALU.add),
                   r=[R_t[i]], w=[R_t[i]])
                if isinstance(out_ap, list):
                    for (rows_, oap) in out_ap:
                        op("dve", lambda h: h.tensor_tensor(out=oap, in0=t1[i][rows_, :], in1=rs2[i][rows_, :], op=ALU.mult),
                           r=[R_t[i], R_rs2[i]], w=[R_out])
                else:
                    op("dve", lambda h: h.tensor_tensor(out=out_ap, in0=t1[i], in1=rs2[i], op=ALU.mult),
                       r=[R_t[i], R_rs2[i]], w=[R_out])

            for j in range(4):
                (wq, R_wq), (ws, R_ws) = load_groups([j, 4 + j])
                for tb in range(NTB):
                    ts = slice(tb * TB, (tb + 1) * TB)
                    bq, bs = next_banks(2)
                    proj(wq, R_wq, tb, bq)
                    proj(ws, R_ws, tb, bs)
                    qk_epilogue(tb, bq, bs, V_GQ, qA[:, j, ts], R_qA[tb])
            (wq, R_wq), (ws, R_ws) = load_groups([8, 9])
            for tb in range(NTB):
                ts = slice(tb * TB, (tb + 1) * TB)
                bq, bs = next_banks(2)
                proj(wq, R_wq, tb, bq)
                proj(ws, R_ws, tb, bs)
                qk_epilogue(tb, bq, bs, V_GK, [(slice(0, 64), kA[0][0:64, 0, ts]), (slice(64, 128), kA[0][64:128, 1, ts])],
                            R_kvA[0])
            (wv, R_wv), = load_groups([10])
            for tg in range(4):
                (bv,) = next_banks(1)
                for tt in range(4):
                    tok = slice((tg * 4 + tt) * 128, (tg * 4 + tt + 1) * 128)
                    for kc in range(8):
                        op("pe", lambda h: h.matmul(PB[bv][:, tt * 128:(tt + 1) * 128], lhsT=hT[:, kc, tok],
                                                    rhs=wv[:, kc, :], start=(kc == 0), stop=(kc == 7)),
                           r=[R_wv, R_h[tg]], w=[R_PB[bv]])
                src = PB[bv].rearrange("p (t k d) -> p t k d", t=4, k=2)
                op("act", lambda h: h.activation(out=vA[0][:, tg * 4:(tg + 1) * 4, 0:64], in_=src[:, :, 0, :],
                                                 func=AF.Identity), r=[R_PB[bv]], w=[R_kvA[0]])
                op("act", lambda h: h.activation(out=vA[0][:, tg * 4:(tg + 1) * 4, 129:193], in_=src[:, :, 1, :],
                                                 func=AF.Identity), r=[R_PB[bv]], w=[R_kvA[0]])

            def latent(groups, gcol, n_feat, out_fn, R_out_fn):
                ws_ = load_groups(groups)
                ng = len(groups)
                for tb in range(NTB):
                    ts = slice(tb * TB, (tb + 1) * TB)
                    i = it[0] % NT1
                    it[0] += 1
                    banks = next_banks(ng)
                    for gi in range(ng):
                        proj(ws_[gi][0], ws_[gi][1], tb, banks[gi])
                    sb = next_sbank()
                    for gi in range(ng):
                        b = banks[gi]
                        op("act", lambda h: h.activation(out=sq2[i][:, gi, :], in_=PB[b], func=AF.Square),
                           r=[R_PB[b]], w=[R_sq2[i]])
                        op("act", lambda h: h.activation(out=cqf[i][:, gi, :], in_=PB[b], func=AF.Identity),
                           r=[R_PB[b]], w=[R_cqf[i]])
                        op("pe", lambda h: h.matmul(PB[sb], lhsT=ones_bf, rhs=sq2[i][:, gi, :],
                                                    start=(gi == 0), stop=(gi == ng - 1)),
                           r=[R_sq2[i], R_const], w=[R_PB[sb]])
                    rstd_from_ss(PB[sb], R_PB[sb], n_feat, rs2[i], R_rs2[i])
                    for gi in range(ng):
                        op("dve", lambda h: h.scalar_tensor_tensor(
                            out=out_fn(gi, ts), in0=cqf[i][:, gi, :], scalar=vcol(l, gcol + gi), in1=rs2[i],
                            op0=ALU.mult, op1=ALU.mult), r=[R_cqf[i], R_rs2[i], R_vec], w=[R_out_fn(tb)])

            latent([11, 12, 13], V_GCQ, 384, lambda gi, ts: cqn[:, gi, ts], lambda tb: R_cqn[tb])
            latent([14, 15], V_GCKV, 256, lambda gi, ts: ckvn[0][:, gi, ts], lambda tb: R_ckvn[0])

            (wq, R_wq), (ws, R_ws) = load_groups([16, 17])
            for tb in range(NTB):
                ts = slice(tb * TB, (tb + 1) * TB)
                i = it[0] % NT1
                it[0] += 1
                bq, bs = next_banks(2)
                proj(wq, R_wq, tb, bq, M=96)
                proj(ws, R_ws, tb, bs, M=96)
                tabB, R_tab = load_tab(1, tb, rows=slice(64, 96))
                op("dve", lambda h: h.tensor_tensor(out=t1[i][64:96, :], in0=PB[bq][64:96, :],
                                                    in1=tabB[64:96, 0, :], op=ALU.mult),
                   r=[R_PB[bq], R_tab], w=[R_t[i]])
                op("dve", lambda h: h.tensor_tensor(out=t2[i][64:96, :], in0=PB[bs][64:96, :],
                                                    in1=tabB[64:96, 1, :], op=ALU.mult),
                   r=[R_PB[bs], R_tab], w=[R_t[i]])
                op("dve", lambda h: h.tensor_tensor(out=krT[64:96, ts], in0=t1[i][64:96, :],
                                                    in1=t2[i][64:96, :], op=ALU.add),
                   r=[R_t[i]], w=[R_kr])

            if is_sample:
                g_ = gin[l]
                dma("pool", g_[0][0:64, :], kA[0][0:64, 0, :], r=[R_kvA[0]], w=[R_gin[l][0]])
                dma("pool", g_[0][64:128, :], kA[0][64:128, 1, :], r=[R_kvA[0]], w=[R_gin[l][0]])
                dma("pool", g_[3].rearrange("p (t c) -> p t c", t=16), vA[0], r=[R_kvA[0]], w=[R_gin[l][3]])
                dma("pool", g_[1].rearrange("(c p) t -> p c t", p=128), ckvn[0],
                    r=[R_ckvn[0]], w=[R_gin[l][1]])
                dma("pool", g_[2], krT[64:96, :], r=[R_kr], w=[R_gin[l][2]])
                for p_ in range(NGP):
                    tk.custom("pool", lambda h: h.collective_compute(
                        "AllGather", ALU.bypass, replica_groups=[[0, 1, 2, 3], [4, 5, 6, 7]],
                        ins=[gin[l][p_].opt()], outs=[gout[l][p_].opt()]), cc_sems[l][p_],
                        r=[R_gin[l][p_]], w=[R_gout[l][p_]])

            ar.reset(p1_mark)
            tk.barrier()
            tabQ = [ar.alloc([128, 2, TB], F32) for _ in range(1)]
            R_tabQ = [Region(f"tabQ{i}") for i in range(1)]
            NPT = 5
            pT = [ar.alloc([128, TB], BF16) for _ in range(NPT)]
            R_pT = [Region(f"pT{i}") for i in range(NPT)]
            NT2 = 1
            rrow = [ar.alloc([128, TB], F32) for _ in range(NT2)]
            R_rrow = [Region(f"rrow{i}") for i in range(NT2)]
            rb = [ar.alloc([128, TB], F32) for _ in range(NT2)]
            R_rb = [Region(f"rb{i}") for i in range(NT2)]
            Kh = [ar.alloc([128, T], BF16) for _ in range(2)]
            Vh = [ar.alloc([128, 16, 192], BF16) for _ in range(2)]
            R_KVh = [Region(f"KVh{i}") for i in range(2)]
            Qh = [ar.alloc([128, T], BF16) for _ in range(2)]
            R_Qh = [Region(f"Qh{i}") for i in range(2)]
            wq_st = Stream("wuq", 2, [128, 2, 3, 96],
                           [WB["uq"][l, h_ * 128:(h_ + 1) * 128, :].rearrange("p (v k m) -> p v k m", v=2, k=3) for h_ in range(8)],
                           [R_w[("uq", l)]], ahead=1)
            wkv_st = Stream("wukv", 2, [128, 2, 128], [wsrc("ukv", l, h_, 2) for h_ in range(8)],
                            [R_w[("ukv", l)]], ahead=1)
            tq = [ar.alloc([128, TB], F32) for _ in range(2)]
            R_tq = [Region(f"tq{i}") for i in range(2)]
            for i in range(2):
                op("pool", lambda h: h.memset(Kh[i][96:128, :], 0.0), w=[R_KVh[i]])
                op("pool", lambda h: h.memset(Qh[i][96:128, :], 0.0), w=[R_Qh[i]])
                op("pool", lambda h: h.memset(Vh[i][:, :, 0:64], 0.0), w=[R_KVh[i]])
                op("pool", lambda h: h.memset(Vh[i][:, :, 0:1], 1.0), w=[R_KVh[i]])
                op("pool", lambda h: h.memset(Vh[i][:, :, 128:192], 0.0), w=[R_KVh[i]])
                op("pool", lambda h: h.memset(Vh[i][:, :, 128:129], 1.0), w=[R_KVh[i]])

            OB = [0, 1, 2, 3]
            SB = [4, 5, 6, 7]
            DEPTH = 3
            BB = 6
            GB = 7
            GB2 = 6
            s_i = [0]
            n_i = [0]

            def attention_core(q_fn, R_q, k_fn, v_fn, R_kv, hi, scale, c, last):
                steps = [(qb, kt) for qb in range(NTB) for kt in range(16 if not os.environ.get("KDBG_NOATT") else 1)]
                M = 128 if hi else 65
                queue = []
                for st_ in steps + [None] * DEPTH:
                    if st_ is not None:
                        qb, kt = st_
                        si = s_i[0]
                        s_i[0] += 1
                        sb_ = SB[si % len(SB)]
                        pi = si % NPT
                        qs = slice(qb * TB, (qb + 1) * TB)
                        op("pe", lambda h: h.matmul(PB[sb_], lhsT=k_fn(kt), rhs=q_fn(qs), start=True, stop=True),
                           r=[R_kv, R_q[qb]], w=[R_PB[sb_]])
                        op("act", lambda h: h.activation(out=pT[pi], in_=PB[sb_], func=AF.Exp, bias=0.0, scale=scale),
                           r=[R_PB[sb_]], w=[R_pT[pi]])
                        queue.append((qb, kt, pi))
                    if queue and (len(queue) > DEPTH or st_ is None):
                        pqb, pkt, ppi = queue.pop(0)
                        op("pe", lambda h: h.matmul(PB[OB[pqb]][0:M, :], lhsT=v_fn(pkt), rhs=pT[ppi],
                                                    start=(c == 0 and pkt == 0), stop=(last and pkt == 15)),
                           r=[R_kv, R_pT[ppi]], w=[R_PB[OB[pqb]]])

            def normalize(hi, out_fn, R_out_fn):
                for qb in range(NTB):
                    qs = slice(qb * TB, (qb + 1) * TB)
                    i = n_i[0] % NT2
                    n_i[0] += 1
                    ob = OB[qb]
                    if not hi:
                        op("act", lambda h: h.activation(out=rrow[i][64:65, :], in_=PB[ob][64:65, :], func=AF.Ln),
                           r=[R_PB[ob]], w=[R_rrow[i]])
                        op("act", lambda h: h.activation(out=rrow[i][64:65, :], in_=rrow[i][64:65, :], func=AF.Exp,
                                                         scale=-1.0), r=[R_rrow[i]], w=[R_rrow[i]])
                        op("pe", lambda h: h.matmul(PB[BB][0:64, :], lhsT=ones_lo[64:65, 0:64], rhs=rrow[i][64:65, :],
                                                    start=True, stop=True), r=[R_rrow[i], R_const], w=[R_PB[BB]])
                        rows = slice(0, 64)
                    else:
                        op("act", lambda h: h.activation(out=rrow[i][0:1, :], in_=PB[ob][0:1, :], func=AF.Ln),
                           r=[R_PB[ob]], w=[R_rrow[i]])
                        op("act", lambda h: h.activation(out=rrow[i][0:1, :], in_=rrow[i][0:1, :], func=AF.Exp,
                                                         scale=-1.0), r=[R_rrow[i]], w=[R_rrow[i]])
                        op("pe", lambda h: h.matmul(PB[BB], lhsT=ones_hi[0:1, :], rhs=rrow[i][0:1, :],
                                                    start=True, stop=True), r=[R_rrow[i], R_const], w=[R_PB[BB]])
                        rows = slice(64, 128)
                    op("dve", lambda h: h.tensor_copy(out=rb[i][rows, :], in_=PB[BB][rows, :]),
                       r=[R_PB[BB]], w=[R_rb[i]])
                    op("dve", lambda h: h.tensor_tensor(out=out_fn(rows, qs), in0=PB[ob][rows, :], in1=rb[i][rows, :],
                                                        op=ALU.mult), r=[R_PB[ob], R_rb[i]], w=[R_out_fn(qb)])

            ld_i = [0]
            for hA in range(8):
                kv = hA // 4
                j = hA % 4
                rows = slice(kv * 64, kv * 64 + 64)
                for c in range(nchunks):
                    if is_sample:
                        bi = 0
                        dma("sp", kA[bi][0:64, 0, :], gout[l][0][c * 128:c * 128 + 64, :], r=[R_gout[l][0]], w=[R_kvA[bi]])
                        dma("sp", kA[bi][64:128, 1, :], gout[l][0][c * 128 + 64:(c + 1) * 128, :], r=[R_gout[l][0]],
                            w=[R_kvA[bi]])
                        dma("sp", vA[bi], gout[l][3][c * 128:(c + 1) * 128, :].rearrange("p (t c) -> p t c", t=16),
                            r=[R_gout[l][3]], w=[R_kvA[bi]])
                    else:
                        bi = 0
                    if kv == 0:
                        v_fn = (lambda kt, bi=bi: vA[bi][:, kt, 0:128])
                    else:
                        v_fn = (lambda kt, bi=bi: vA[bi][:, kt, 65:193])
                    attention_core(
                        q_fn=lambda qs, j=j: qA[:, j, qs], R_q=R_qA,
                        k_fn=lambda kt, bi=bi, kv=kv: kA[bi][:, kv, kt * 128:(kt + 1) * 128],
                        v_fn=v_fn, R_kv=R_kvA[bi], hi=True, scale=SCALE_A,
                        c=c, last=(c == nchunks - 1))
                normalize(kv == 1, lambda rows_, qs, j=j: attnA[rows_, j, qs], lambda qb: R_attnA[qb])

            if not is_sample:
                for i in range(2):
                    op("pool", lambda h: h.tensor_copy(out=Kh[i][64:96, :], in_=krT[64:96, :]),
                       r=[R_kr], w=[R_KVh[i]])
            g_i = [0]
            for hB in range(8):
                hi = (hB % 2 == 1)
                rows = slice(64, 128) if hi else slice(0, 64)
                wq_t, R_wq_t = wq_st.get()
                wkv_t, R_wkv_t = wkv_st.get()
                qi = hB % 2
                for qb in range(NTB):
                    qs = slice(qb * TB, (qb + 1) * TB)
                    tqi = 0
                    g_i[0] += 1
                    dma(WQ, tabQ[tqi][64:96], tabs[tabsel, 1, qb, 64:96, :].rearrange("p (k t) -> p k t", k=2),
                        r=[R_tabs], w=[R_tabQ[tqi]])
                    for kc in range(3):
                        op("pe", lambda h: h.matmul(PB[GB][0:96, :], lhsT=wq_t[:, 0, kc, :], rhs=cqn[:, kc, qs],
                                                    start=(kc == 0), stop=(kc == 2)),
                           r=[R_wq_t, R_cqn[qb]], w=[R_PB[GB]])
                    op("dve", lambda h: h.tensor_copy(out=Qh[qi][0:64, qs], in_=PB[GB][0:64, :]),
                       r=[R_PB[GB]], w=[R_Qh[qi]])
                    op("dve", lambda h: h.tensor_tensor(out=tq[0][64:96, :], in0=PB[GB][64:96, :],
                                                        in1=tabQ[tqi][64:96, 0, :], op=ALU.mult),
                       r=[R_PB[GB], R_tabQ[tqi]], w=[R_tq[0]])
                    for kc in range(3):
                        op("pe", lambda h: h.matmul(PB[GB2][0:96, :], lhsT=wq_t[:, 1, kc, :], rhs=cqn[:, kc, qs],
                                                    start=(kc == 0), stop=(kc == 2)),
                           r=[R_wq_t, R_cqn[qb]], w=[R_PB[GB2]])
                    op("dve", lambda h: h.tensor_tensor(out=tq[1][64:96, :], in0=PB[GB2][64:96, :],
                                                        in1=tabQ[tqi][64:96, 1, :], op=ALU.mult),
                       r=[R_PB[GB2], R_tabQ[tqi]], w=[R_tq[1]])
                    op("dve", lambda h: h.tensor_tensor(out=Qh[qi][64:96, qs], in0=tq[0][64:96, :],
                                                        in1=tq[1][64:96, :], op=ALU.add),
                       r=[R_tq[0], R_tq[1]], w=[R_Qh[qi]])
                for c in range(nchunks):
                    ki = g_i[0] % 2
                    g_i[0] += 1
                    if is_sample:
                        bi = 0
                        dma("sp", ckvn[bi], gout[l][1][c * 256:(c + 1) * 256, :].rearrange("(c p) t -> p c t", p=128),
                            r=[R_gout[l][1]], w=[R_ckvn[bi]])
                        dma("sp", Kh[ki][64:96, :], gout[l][2][c * 32:(c + 1) * 32, :], r=[R_gout[l][2]], w=[R_KVh[ki]])
                    else:
                        bi = 0
                    for kb in range(4):
                        GBK = GB if kb % 2 == 0 else GB2
                        ks = slice(kb * TB, (kb + 1) * TB)
                        for kc in range(2):
                            op("pe", lambda h: h.matmul(PB[GBK][0:64, :], lhsT=wkv_t[:, kc, 0:64], rhs=ckvn[bi][:, kc, ks],
                                                        start=(kc == 0), stop=(kc == 1)),
                               r=[R_wkv_t, R_ckvn[bi]], w=[R_PB[GBK]])
                        op("dve", lambda h: h.tensor_copy(out=Kh[ki][0:64, ks], in_=PB[GBK][0:64, :]),
                           r=[R_PB[GBK]], w=[R_KVh[ki]])
                    for tg in range(2):
                        GBV = GB if tg % 2 == 0 else GB2
                        for tt in range(8):
                            kt = tg * 8 + tt
                            for kc in range(2):
                                op("pe", lambda h: h.matmul(PB[GBV][:, tt * 64:(tt + 1) * 64],
                                                            lhsT=ckvn[bi][:, kc, kt * 128:(kt + 1) * 128],
                                                            rhs=wkv_t[:, kc, 64:128], start=(kc == 0), stop=(kc == 1)),
                                   r=[R_wkv_t, R_ckvn[bi]], w=[R_PB[GBV]])
                        src = PB[GBV].rearrange("p (t d) -> p t d", t=8)
                        dst = Vh[ki][:, tg * 8:(tg + 1) * 8, 64:128]
                        op("dve", lambda h: h.tensor_copy(out=dst, in_=src),
                           r=[R_PB[GBV]], w=[R_KVh[ki]])
                    if hi:
                        v_fn = (lambda kt, ki=ki: Vh[ki][:, kt, 0:128])
                    else:
                        v_fn = (lambda kt, ki=ki: Vh[ki][:, kt, 64:192])
                    attention_core(
                        q_fn=lambda qs, qi=qi: Qh[qi][:, qs], R_q=[R_Qh[qi]] * NTB,
                        k_fn=lambda kt, ki=ki: Kh[ki][:, kt * 128:(kt + 1) * 128],
                        v_fn=v_fn, R_kv=R_KVh[ki], hi=True, scale=SCALE_B,
                        c=c, last=(c == nchunks - 1))
                normalize(hi, lambda rows_, qs, jj=hB // 2: attnB[rows_, jj, qs], lambda qb: R_attnB[qb])

            ar.reset(attn_mark)
            tk.barrier()
            mT = ar.alloc([128, 8, T], BF16)
            R_m = [Region(f"m{tb}") for tb in range(NTB)]
            p3_mark = ar.mark()
            hT = ar.alloc([128, 8, T], BF16)
            R_h = [Region(f"h{tb}") for tb in range(NTB)]
            sq = ar.alloc([128, 8, TB], BF16)
            R_sq = Region("sq")
            rs = ar.alloc([128, TB], F32)
            R_rs = Region("rs")
            norm_x_to_h(l, V_GPRE, hT, R_h, (sq, R_sq, rs, R_rs, PB[7], R_PB[7]))
            woa_st = Stream("woa", 2, [128, 4, 128], [wsrc("oa", l, o_, 4) for o_ in range(8)], [R_w[("oa", l)]], ahead=1)
            wob_st = Stream("wob", 2, [128, 4, 128], [wsrc("ob", l, o_, 4) for o_ in range(8)], [R_w[("ob", l)]], ahead=1)
            wg_st = Stream("wg", 4, [128, 8, 128],
                           [wsrc("in", l, (18 if k_ == 0 else 26) + o_, 8) for o_ in range(8) for k_ in range(2)],
                           [R_w[("in", l)]], ahead=2)
            ga = [ar.alloc([128, TB], F32) for _ in range(2)]
            gb = [ar.alloc([128, TB], F32) for _ in range(2)]
            R_g = [Region(f"g{i}") for i in range(2)]
            e_i = 0
            for o in range(8):
                woa_t, R_woa = woa_st.get()
                wob_t, R_wob = wob_st.get()
                wga_t, R_wga = wg_st.get()
                wgb_t, R_wgb = wg_st.get()
                for tb in range(NTB):
                    ts = slice(tb * TB, (tb + 1) * TB)
                    i = e_i % 2
                    b0 = (e_i % 2) * 4
                    e_i += 1
                    for j in range(4):
                        op("pe", lambda h: h.matmul(PB[b0], lhsT=woa_t[:, j, :], rhs=attnA[:, j, ts],
                                                    start=(j == 0), stop=(j == 3)),
                           r=[R_woa, R_attnA[tb]], w=[R_PB[b0]])
                    for j in range(4):
                        op("pe", lambda h: h.matmul(PB[b0 + 1], lhsT=wob_t[:, j, :], rhs=attnB[:, j, ts],
                                                    start=(j == 0), stop=(j == 3)),
                           r=[R_wob, R_attnB[tb]], w=[R_PB[b0 + 1]])
                    for kc in range(8):
                        op("pe", lambda h: h.matmul(PB[b0 + 2], lhsT=wga_t[:, kc, :], rhs=hT[:, kc, ts],
                                                    start=(kc == 0), stop=(kc == 7)),
                           r=[R_wga, R_h[tb]], w=[R_PB[b0 + 2]])
                    for kc in range(8):
                        op("pe", lambda h: h.matmul(PB[b0 + 3], lhsT=wgb_t[:, kc, :], rhs=hT[:, kc, ts],
                                                    start=(kc == 0), stop=(kc == 7)),
                           r=[R_wgb, R_h[tb]], w=[R_PB[b0 + 3]])
                    op("act", lambda h: h.activation(out=ga[i], in_=PB[b0 + 2], func=AF.Sigmoid,
                                                     bias=vcol(l, V_BG + o), scale=1.0),
                       r=[R_PB[b0 + 2], R_vec], w=[R_g[i]])
                    op("act", lambda h: h.activation(out=gb[i], in_=PB[b0 + 3], func=AF.Sigmoid,
                                                     bias=vcol(l, V_BG + 8 + o), scale=1.0),
                       r=[R_PB[b0 + 3], R_vec], w=[R_g[i]])
                    op("dve", lambda h: h.tensor_tensor(out=ga[i], in0=ga[i], in1=PB[b0], op=ALU.mult),
                       r=[R_g[i], R_PB[b0]], w=[R_g[i]])
                    op("dve", lambda h: h.tensor_tensor(out=gb[i], in0=gb[i], in1=PB[b0 + 1], op=ALU.mult),
                       r=[R_g[i], R_PB[b0 + 1]], w=[R_g[i]])
                    op("dve", lambda h: h.tensor_tensor(out=mT[:, o, ts], in0=ga[i], in1=gb[i], op=ALU.add),
                       r=[R_g[i]], w=[R_m[tb]])

            ar.reset(p3_mark)
            tk.barrier()
            wout_t = ar.alloc([128, 8, 8, 128], BF16)
            R_wout = Region("wout")
            for o in range(8):
                dma(WQ, wout_t[:, o], wsrc("out", l, o, 8), r=[R_w[("out", l)]], w=[R_wout])
            mix = [ar.alloc([128, 8, TB], F32) for _ in range(2)]
            R_mix = [Region(f"mix{i}") for i in range(2)]
            sq3 = [ar.alloc([128, 8, TB], BF16) for _ in range(2)]
            R_sq3 = [Region(f"sq3{i}") for i in range(2)]
            rs3 = [ar.alloc([128, TB], F32) for _ in range(2)]
            R_rs3 = [Region(f"rs3{i}") for i in range(2)]
            tt3 = [ar.alloc([128, TB], F32) for _ in range(2)]
            R_tt3 = [Region(f"tt3{i}") for i in range(2)]
            bk = 0
            for tb in range(NTB):
                ts = slice(tb * TB, (tb + 1) * TB)
                i = tb % 2
                sbk = 6 + (tb % 2)
                for o in range(8):
                    b = bk % 6
                    bk += 1
                    for kc in range(8):
                        op("pe", lambda h: h.matmul(PB[b], lhsT=wout_t[:, o, kc, :], rhs=mT[:, kc, ts],
                                                    start=(kc == 0), stop=(kc == 7)),
                           r=[R_wout, R_m[tb]], w=[R_PB[b]])
                    op("act", lambda h: h.activation(out=mix[i][:, o, :], in_=PB[b], func=AF.Identity),
                       r=[R_PB[b]], w=[R_mix[i]])
                    op("act", lambda h: h.activation(out=sq3[i][:, o, :], in_=PB[b], func=AF.Square),
                       r=[R_PB[b]], w=[R_sq3[i]])
                    op("pe", lambda h: h.matmul(PB[sbk], lhsT=ones_bf, rhs=sq3[i][:, o, :],
                                                start=(o == 0), stop=(o == 7)),
                       r=[R_sq3[i], R_const], w=[R_PB[sbk]])
                rstd_from_ss(PB[sbk], R_PB[sbk], D, rs3[i], R_rs3[i])
                for o in range(8):
                    op("dve", lambda h: h.scalar_tensor_tensor(
                        out=tt3[i], in0=mix[i][:, o, :], scalar=vcol(l, V_GPOST + o), in1=rs3[i],
                        op0=ALU.mult, op1=ALU.mult), r=[R_mix[i], R_rs3[i], R_vec], w=[R_tt3[i]])
                    op("dve", lambda h: h.tensor_tensor(out=xT[:, o, ts], in0=xT[:, o, ts], in1=tt3[i], op=ALU.add),
                       r=[R_tt3[i], R_x[tb]], w=[R_x[tb]])

            ar.reset(base_mark)
            tk.barrier()
            hT = ar.alloc([128, 8, T], BF16)
            R_h = [Region(f"h{tb}") for tb in range(NTB)]
            sq = ar.alloc([128, 8, TB], BF16)
            R_sq = Region("sq")
            rs = ar.alloc([128, TB], F32)
            R_rs = Region("rs")
            norm_x_to_h(l, V_GFPRE, hT, R_h, (sq, R_sq, rs, R_rs, PB[7], R_PB[7]))
            if is_sample:
                op("dve", lambda h: h.tensor_copy(out=hx[:, 0:8], in_=hT[:, :, 0]), r=[R_h[0]], w=[R_hx])
                op("dve", lambda h: h.tensor_copy(out=hx[:, 8:16], in_=hT[:, :, T - 1]), r=[R_h[3]], w=[R_hx])
                dma("pool", hgin[l], hx, r=[R_hx], w=[R_hgin[l]])
                tk.custom("pool", lambda h: h.collective_compute(
                    "AllGather", ALU.bypass, replica_groups=[[0, 1, 2, 3], [4, 5, 6, 7]],
                    ins=[hgin[l].opt()], outs=[hgout[l].opt()]), cc_h[l], r=[R_hgin[l]], w=[R_hgout[l]])
                dma("sp", hg, hgout[l].rearrange("(r p) c -> p r c", p=128), r=[R_hgout[l]], w=[R_hg])
                for side in range(2):
                    cs = slice(8, 16) if side == 0 else slice(0, 8)
                    op("dve", lambda h: h.tensor_scalar(out=hacc[:, side, :], in0=hg[:, 0, cs],
                                                        scalar1=nbr_sb[:, 4 * side:4 * side + 1], scalar2=None,
                                                        op0=ALU.mult), r=[R_hg, R_vec], w=[R_halo])
                    for r_ in range(1, 4):
                        op("dve", lambda h: h.scalar_tensor_tensor(
                            out=hacc[:, side, :], in0=hg[:, r_, cs], scalar=nbr_sb[:, 4 * side + r_:4 * side + r_ + 1],
                            in1=hacc[:, side, :], op0=ALU.mult, op1=ALU.add), r=[R_hg, R_vec, R_halo], w=[R_halo])
                    op("dve", lambda h: h.tensor_copy(out=halo_b[:, side, :], in_=hacc[:, side, :]),
                       r=[R_halo], w=[R_halo])
            HT_ = 1024
            gbuf = ar.alloc([128, NCH, HT_], BF16)
            R_gbuf = Region("gbuf")
            p4_mark = ar.mark()
            for hf in range(2):
                ar.reset(p4_mark)
                if hf == 1:
                    tk.barrier()
                t0 = hf * HT_
                wup_st = Stream("wup", 4, [128, 8, 128],
                                [wsrc("up", l, c_ + w2 * NCH, 8) for c_ in range(NCH) for w2 in range(2)],
                                [R_w[("up", l)]], ahead=2)
                abuf = [[ar.alloc([128, HT_ + 2], F32) for _ in range(2)] for _ in range(2)]
                R_ab = [[Region(f"ab{w_}{i}") for i in range(2)] for w_ in range(2)]
                ubuf = [[ar.alloc([128, HT_], F32) for _ in range(2)] for _ in range(2)]
                R_ub = [[Region(f"ub{w_}{i}") for i in range(2)] for w_ in range(2)]
                out_col = 0 if hf == 0 else HT_ + 1
                if not is_sample:
                    for w_ in range(2):
                        for i in range(2):
                            op("pool", lambda h: h.memset(abuf[w_][i][:, out_col:out_col + 1], 0.0), w=[R_ab[w_][i]])
                halo_tok = (t0 + HT_) if hf == 0 else (t0 - 1)
                halo_col = (HT_ + 1) if hf == 0 else 0
                halo_tb = halo_tok // TB
                bk = 0
                for c in range(NCH):
                    par = c % 2
                    for w_ in range(2):
                        wt, R_wt = wup_st.get()
                        g = c + w_ * NCH
                        a_ = abuf[w_][par]
                        for blk in range(2):
                            tbi = (t0 // TB) + blk
                            ts = slice(t0 + blk * TB, t0 + (blk + 1) * TB)
                            b = bk % 6
                            bk += 1
                            for kc in range(8):
                                op("pe", lambda h: h.matmul(PB[b], lhsT=wt[:, kc, :], rhs=hT[:, kc, ts],
                                                            start=(kc == 0), stop=(kc == 7)),
                                   r=[R_wt, R_h[tbi]], w=[R_PB[b]])
                            op("act", lambda h: h.activation(out=a_[:, 1 + blk * TB:1 + (blk + 1) * TB], in_=PB[b],
                                                             func=AF.Identity), r=[R_PB[b]], w=[R_ab[w_][par]])
                        hb = 6 + (bk % 2)
                        for kc in range(8):
                            op("pe", lambda h: h.matmul(PB[hb][:, 0:1], lhsT=wt[:, kc, :],
                                                        rhs=hT[:, kc, halo_tok:halo_tok + 1],
                                                        start=(kc == 0), stop=(kc == 7)),
                               r=[R_wt, R_h[halo_tb]], w=[R_PB[hb]])
                        op("act", lambda h: h.activation(out=a_[:, halo_col:halo_col + 1], in_=PB[hb][:, 0:1],
                                                         func=AF.Identity), r=[R_PB[hb]], w=[R_ab[w_][par]])
                        if is_sample:
                            for kc in range(8):
                                op("pe", lambda h: h.matmul(PB[hb][:, 8:9], lhsT=wt[:, kc, :],
                                                            rhs=halo_b[:, hf, kc:kc + 1],
                                                            start=(kc == 0), stop=(kc == 7)),
                                   r=[R_wt, R_halo], w=[R_PB[hb]])
                            op("act", lambda h: h.activation(out=a_[:, out_col:out_col + 1], in_=PB[hb][:, 8:9],
                                                             func=AF.Identity), r=[R_PB[hb]], w=[R_ab[w_][par]])
                        u_ = ubuf[w_][par]
                        cw = lambda jj: vcol(l, V_CW + jj * 2 * NCH + g)
                        op("act", lambda h: h.activation(out=u_, in_=a_[:, 1:HT_ + 1], func=AF.Identity,
                                                         bias=vcol(l, V_CB + g), scale=cw(1)),
                           r=[R_ab[w_][par], R_vec], w=[R_ub[w_][par]])
                        op("dve", lambda h: h.scalar_tensor_tensor(
                            out=u_, in0=a_[:, 0:HT_], scalar=cw(0), in1=u_, op0=ALU.mult, op1=ALU.add),
                           r=[R_ab[w_][par], R_ub[w_][par], R_vec], w=[R_ub[w_][par]])
                        op("dve", lambda h: h.scalar_tensor_tensor(
                            out=u_, in0=a_[:, 2:HT_ + 2], scalar=cw(2), in1=u_, op0=ALU.mult, op1=ALU.add),
                           r=[R_ab[w_][par], R_ub[w_][par], R_vec], w=[R_ub[w_][par]])
                    ug, uv = ubuf[0][par], ubuf[1][par]
                    op("act", lambda h: h.activation(out=ug, in_=ug, func=AF.Gelu_apprx_tanh),
                       r=[R_ub[0][par]], w=[R_ub[0][par]])
                    op("dve", lambda h: h.tensor_tensor(out=gbuf[:, c, :], in0=ug, in1=uv, op=ALU.mult),
                       r=[R_ub[0][par], R_ub[1][par]], w=[R_gbuf])
                ar.reset(p4_mark)
                tk.barrier()
                wd_st = Stream("wd", 2, [128, NCH, 128], [wsrc("down", l, o_, NCH) for _b in range(2) for o_ in range(8)],
                               [R_w[("down", l)]], ahead=1)
                fsb = ar.alloc([128, 8, TB], F32)
                R_fsb = Region("fsb")
                sq4 = [ar.alloc([128, TB], BF16) for _ in range(2)]
                R_sq4 = [Region(f"sq4{i}") for i in range(2)]
                rs4 = ar.alloc([128, TB], F32)
                R_rs4 = Region("rs4")
                tt4 = ar.alloc([128, TB], F32)
                R_tt4 = Region("tt4")
                bk = 0
                q_i = 0
                for blk in range(2):
                    bs_ = slice(blk * TB, (blk + 1) * TB)
                    tbi = (t0 // TB) + blk
                    ts = slice(t0 + blk * TB, t0 + (blk + 1) * TB)
                    sbk = 6 + blk
                    for o in range(8):
                        wd_t, R_wd = wd_st.get()
                        b = bk % 6
                        bk += 1
                        for c in range(NCH):
                            op("pe", lambda h: h.matmul(PB[b], lhsT=wd_t[:, c, :], rhs=gbuf[:, c, bs_],
                                                        start=(c == 0), stop=(c == NCH - 1)),
                               r=[R_wd, R_gbuf], w=[R_PB[b]])
                        op("act", lambda h: h.activation(out=fsb[:, o, :], in_=PB[b], func=AF.Identity),
                           r=[R_PB[b]], w=[R_fsb])
                        qi_ = q_i % 2
                        q_i += 1
                        op("act", lambda h: h.activation(out=sq4[qi_], in_=PB[b], func=AF.Square),
                           r=[R_PB[b]], w=[R_sq4[qi_]])
                        op("pe", lambda h: h.matmul(PB[sbk], lhsT=ones_bf, rhs=sq4[qi_],
                                                    start=(o == 0), stop=(o == 7)),
                           r=[R_sq4[qi_], R_const], w=[R_PB[sbk]])
                    rstd_from_ss(PB[sbk], R_PB[sbk], D, rs4, R_rs4)
                    for o in range(8):
                        op("dve", lambda h: h.scalar_tensor_tensor(
                            out=tt4, in0=fsb[:, o, :], scalar=vcol(l, V_GFPOST + o), in1=rs4,
                            op0=ALU.mult, op1=ALU.mult), r=[R_fsb, R_rs4, R_vec], w=[R_tt4])
                        op("dve", lambda h: h.tensor_tensor(out=xT[:, o, ts], in0=xT[:, o, ts], in1=tt4,
                                                            op=ALU.add), r=[R_tt4, R_x[tbi]], w=[R_x[tbi]])

        dma(WQ, yout[u].rearrange("p (k t) -> p k t", k=8), xT, r=R_x, w=[R_yout])

    tk.barrier()
    return nc, tk


def _prep_weights(inp):
    sw = lambda d: d ^ 1
    d64 = np.arange(64)
    d32 = np.arange(32)
    cols = []
    for j in range(4):
        cols.append(np.concatenate([j * 64 + d64, (j + 4) * 64 + d64]))
    for j in range(4):
        cols.append(np.concatenate([j * 64 + sw(d64), (j + 4) * 64 + sw(d64)]))
    cols.append(512 + np.concatenate([d64, 64 + d64]))
    cols.append(512 + np.concatenate([sw(d64), 64 + sw(d64)]))
    cols.append(640 + np.arange(128))
    for g in range(3):
        cols.append(768 + g * 128 + np.arange(128))
    for g in range(2):
        cols.append(1152 + g * 128 + np.arange(128))
    cols.append(np.concatenate([1152 + d64, 1408 + d32, 1408 + d32]))
    cols.append(np.concatenate([1152 + d64, 1408 + sw(d32), 1408 + sw(d32)]))
    for g in range(16):
        cols.append(1440 + g * 128 + np.arange(128))
    cols = np.stack(cols)
    w_in = inp["w_in"]
    wi = w_in[:, :, cols]
    wi = wi.reshape(L, 8, 128, NG_IN, 128).transpose(0, 3, 2, 1, 4)
    out = {"w_in": np.ascontiguousarray(wi).reshape(L, NG_IN * 128, 1024)}
    wq = inp["w_uq"].reshape(L, 3, 128, 8, 96)
    idx_sw = np.concatenate([np.arange(64), 64 + sw(d32)])
    wq2 = np.stack([wq, wq[..., idx_sw]], axis=0)
    out["w_uq"] = np.ascontiguousarray(wq2.transpose(1, 4, 3, 0, 2, 5)).reshape(L, 8 * 128, 576)
    wkv = inp["w_ukv"].reshape(L, 2, 128, 8, 128)
    out["w_ukv"] = np.ascontiguousarray(wkv.transpose(0, 3, 2, 1, 4)).reshape(L, 1024, 256)
    woa = inp["w_oa"].reshape(L, 2, 4, 64, 8, 128)
    out["w_oa"] = np.ascontiguousarray(woa.transpose(0, 4, 1, 3, 2, 5)).reshape(L, 1024, 512)
    wob = inp["w_ob"].reshape(L, 4, 128, 8, 128)
    out["w_ob"] = np.ascontiguousarray(wob.transpose(0, 3, 2, 1, 4)).reshape(L, 1024, 512)
    wo = inp["w_out"].reshape(L, 8, 128, 8, 128)
    out["w_out"] = np.ascontiguousarray(wo.transpose(0, 3, 2, 1, 4)).reshape(L, 1024, 1024)
    wu = inp["w_up"].reshape(L, 8, 128, 2 * NCH, 128)
    out["w_up"] = np.ascontiguousarray(wu.transpose(0, 3, 2, 1, 4)).reshape(L, 2 * NCH * 128, 1024)
    wd = inp["w_down"].reshape(L, NCH, 128, 8, 128)
    out["w_down"] = np.ascontiguousarray(wd.transpose(0, 3, 2, 1, 4)).reshape(L, 1024, NCH * 128)
    vec = np.zeros((L, 128, NV), np.float32)
    p = np.arange(128)
    for l in range(L):
        vec[l, :, V_GPRE:V_GPRE + 8] = inp["g_mix_pre"][l].reshape(8, 128).T
        vec[l, :, V_GQ] = inp["g_qa"][l][p % 64]
        vec[l, :, V_GQS] = inp["g_qa"][l][(p % 64) ^ 1]
        vec[l, :, V_GK] = inp["g_ka"][l][p % 64]
        vec[l, :, V_GKS] = inp["g_ka"][l][(p % 64) ^ 1]
        vec[l, :, V_GCQ:V_GCQ + 3] = inp["g_cq"][l].reshape(3, 128).T
        vec[l, :, V_GCKV:V_GCKV + 2] = inp["g_ckv"][l].reshape(2, 128).T
        vec[l, :, V_BG:V_BG + 16] = inp["b_gates"][l].reshape(16, 128).T
        vec[l, :, V_GPOST:V_GPOST + 8] = inp["g_mix_post"][l].reshape(8, 128).T
        vec[l, :, V_GFPRE:V_GFPRE + 8] = inp["g_ffn_pre"][l].reshape(8, 128).T
        vec[l, :, V_CW:V_CW + 132] = inp["conv_w"][l].reshape(3, 2 * NCH, 128).transpose(2, 0, 1).reshape(128, 132)
        vec[l, :, V_CB:V_CB + 44] = inp["conv_b"][l].reshape(2 * NCH, 128).T
        vec[l, :, V_GFPOST:V_GFPOST + 8] = inp["g_ffn_post"][l].reshape(8, 128).T
    out["vecs"] = vec
    return out


_CACHE = {}


def _pidx(q):
    p = np.arange(128)
    out = np.zeros((128, 8), np.float32)
    for which, (dim, nf) in enumerate(((64, 16), (32, 8))):
        d = p % dim
        pair = d // 2
        out[:, 3 * which] = pair % nf
        out[:, 3 * which + 1] = (pair < nf)
        out[:, 3 * which + 2] = np.where(d % 2 == 0, -1.0, 1.0)
    out[:, 6] = q & 1
    out[:, 7] = q >> 1
    return out


def _to_dev(x):
    return x.T.reshape(8, 128, T).transpose(1, 0, 2).reshape(128, 8 * T)


def _from_dev(y):
    return y.reshape(128, 8, T).transpose(1, 0, 2).reshape(D, T).T


def _pidx(q):
    p = np.arange(128)
    out = np.zeros((128, 8), np.float32)
    for which, (dim, nf) in enumerate(((64, 16), (32, 8))):
        d = p % dim
        pair = d // 2
        out[:, 3 * which] = pair % nf
        out[:, 3 * which + 1] = (pair < nf)
        out[:, 3 * which + 2] = np.where(d % 2 == 0, -1.0, 1.0)
    out[:, 6] = q & 1
    out[:, 7] = q >> 1
    return out


def _to_dev(x):
    return x.T.reshape(8, 128, T).transpose(1, 0, 2).reshape(128, 8 * T)


def _from_dev(y):
    return y.reshape(128, 8, T).transpose(1, 0, 2).reshape(D, T).T


def kernel(**inputs):
    inp = {k: np.asarray(v) for k, v in inputs.items()}
    xp = inp["x_prompt"]
    xs = inp["x_sample"]
    wts = _prep_weights(inp)
    in_maps = []
    for c in range(NCORES):
        xin = np.empty((NUNITS, 128, 8 * T), np.float32)
        for u in range(4):
            xin[u] = _to_dev(xp[4 * c + u])
        s, q = c // 4, c % 4
        xin[4] = _to_dev(xs[s, q * T:(q + 1) * T])
        nb = np.zeros((128, 8), np.float32)
        if q - 1 >= 0:
            nb[:, q - 1] = 1.0
        if q + 1 <= 3:
            nb[:, 4 + q + 1] = 1.0
        m = {"xin": xin, "pidx": _pidx(q), "nbr": nb}
        m.update(wts)
        in_maps.append(m)
    if "nc" not in _CACHE:
        _CACHE["nc"] = build_program()[0]
    nc = _CACHE["nc"]
    res = run_bass_kernel_spmd(nc, in_maps, core_ids=list(range(NCORES)))
    yp = np.empty_like(xp)
    ys = np.empty_like(xs)
    for c in range(NCORES):
        y = res.results[c]["yout"]
        for u in range(4):
            yp[4 * c + u] = _from_dev(y[u])
        s, q = c // 4, c % 4
        ys[s, q * T:(q + 1) * T] = _from_dev(y[4])
    return (yp, ys)
```
